# Optimizing a Trainium2 kernel written in Bass

```python
import math
import numpy as np
import jax
import jax.numpy as jnp
from jax import lax

D_MODEL = 2048
BATCH = 16
SEQ = 256
DEPTH = 4
DEC_BATCH = 4
DEC_SEQ = 1024
PAST_LEN = 512

GRID_W = 64
HEAD_DIM = 64
SSM_WIDTH = 768
SSM_GROUP = 16
SSM_GROUPS = SSM_WIDTH // SSM_GROUP
SSM_STATE = 64
WIN_HEADS = 12
WIN_KV_HEADS = 4
WIN_GROUP = WIN_HEADS // WIN_KV_HEADS
WINDOW = 128
WIN_BLOCK = 128
NA_HEADS = 12
NA_ROWS = 8
NA_COLS = 16
NA_QBLK = 16
NA_KBLK = 32
CTX_QBLK = 128
N_BRANCH = 3
BRANCH_W = 768
D_FF = 5632
CONV_W = 3
ROPE_BASE = 10000.0
EPS = 1e-6
NEG_INF = -1e30
ATT_SCALE = HEAD_DIM ** -0.5
WIN_Q = WIN_HEADS * HEAD_DIM
WIN_KV = WIN_KV_HEADS * HEAD_DIM
NA_W = NA_HEADS * HEAD_DIM
IN_SPLITS = (SSM_WIDTH, WIN_Q, WIN_KV, WIN_KV, NA_W, NA_W, NA_W, D_MODEL, D_MODEL, D_MODEL)
N_IN = sum(IN_SPLITS)

kernel_name = 'hybrid_diffusion_prefix_trunk_step'


def rmsnorm(x, g):
    xf = x.astype(jnp.float32)
    y = xf * lax.rsqrt(jnp.mean(xf * xf, axis=-1, keepdims=True) + EPS)
    return (y * g.astype(jnp.float32)).astype(x.dtype)


def modulate(h, shift, scale):
    return h * (1.0 + scale) + shift


def adaln(cond, w_mod, b_mod):
    m = jax.nn.silu(cond) @ w_mod + b_mod
    return jnp.split(m, 6, axis=-1)


def split_in(p):
    idx = [int(i) for i in np.cumsum(IN_SPLITS)[:-1]]
    return jnp.split(p, idx, axis=-1)


def heads(t, n):
    return t.reshape(t.shape[:-1] + (n, HEAD_DIM))


def rope_2d(x):
    L = x.shape[1]
    nf = HEAD_DIM // 4
    half = HEAD_DIM // 2
    pos = jnp.arange(L)
    row = (pos // GRID_W).astype(jnp.float32)
    col = (pos % GRID_W).astype(jnp.float32)
    inv = ROPE_BASE ** (-jnp.arange(nf, dtype=jnp.float32) / nf)
    xf = x.astype(jnp.float32)

    def rot(xh, p):
        ang = p[:, None] * inv[None, :]
        cos = jnp.cos(ang)[:, None, :]
        sin = jnp.sin(ang)[:, None, :]
        a, b = xh[..., :nf], xh[..., nf:]
        return jnp.concatenate([a * cos - b * sin, a * sin + b * cos], axis=-1)

    out = jnp.concatenate([rot(xf[..., :half], row), rot(xf[..., half:], col)], axis=-1)
    return out.astype(x.dtype)


def _lin_recur(e1, e2):
    a1, b1 = e1
    a2, b2 = e2
    return a1 * a2, a2 * b1 + b2


def s5_bidir(u, lam_re, lam_im, log_dt, b_re, b_im, c_re, c_im, d_skip, h0_re, h0_im):
    f32 = jnp.float32
    uf = u.astype(f32)
    uc = uf.astype(jnp.complex64)
    lam = lax.complex(lam_re.astype(f32), lam_im.astype(f32))
    dt = jnp.exp(log_dt.astype(f32))[..., None]
    lam_bar = jnp.exp(lam * dt)
    b_bar = ((lam_bar - 1.0) / lam)[..., None] * lax.complex(b_re.astype(f32), b_im.astype(f32))
    c_mat = lax.complex(c_re.astype(f32), c_im.astype(f32))
    h0 = None if h0_re is None else lax.complex(h0_re.astype(f32), h0_im.astype(f32))

    def scan_dir(i, reverse):
        bu = jnp.einsum('blgn,gpn->blgp', uc, b_bar[i])
        if h0 is not None:
            bu = bu.at[:, -1 if reverse else 0].add(lam_bar[i] * h0[:, i])
        a = jnp.broadcast_to(lam_bar[i], bu.shape)
        _, xs = lax.associative_scan(_lin_recur, (a, bu), reverse=reverse, axis=1)
        return xs

    xf = scan_dir(0, False)
    xb = scan_dir(1, True)
    y = jnp.real(jnp.einsum('blgp,gnp->blgn', xf, c_mat[0]) + jnp.einsum('blgp,gnp->blgn', xb, c_mat[1]))
    y = y + d_skip.astype(f32) * uf
    return y.astype(u.dtype), xf, xb


def ssm_mixer(u, lam_re, lam_im, log_dt, b_re, b_im, c_re, c_im, d_skip, w_glu, b_glu, h0_re, h0_im):
    B, L, _ = u.shape
    y, xf, xb = s5_bidir(u.reshape(B, L, SSM_GROUPS, SSM_GROUP), lam_re, lam_im, log_dt,
                         b_re, b_im, c_re, c_im, d_skip, h0_re, h0_im)
    y = jax.nn.gelu(y.reshape(B, L, SSM_WIDTH))
    return y * jax.nn.sigmoid(y @ w_glu + b_glu), xf, xb


def ctx_self_attention(q, k, v, sink):
    B, L, HK, G, d = q.shape
    nb = L // CTX_QBLK
    qb = jnp.moveaxis(q.reshape(B, nb, CTX_QBLK, HK, G, d), 1, 0)

    def one(qi):
        s = jnp.einsum('bqhgd,bkhd->bhgqk', qi, k).astype(jnp.float32) * ATT_SCALE
        if sink is None:
            p = jax.nn.softmax(s, axis=-1)
        else:
            col = jnp.broadcast_to(sink.astype(jnp.float32)[None, :, :, None, None], s.shape[:-1] + (1,))
            p = jax.nn.softmax(jnp.concatenate([s, col], axis=-1), axis=-1)[..., :-1]
        return jnp.einsum('bhgqk,bkhd->bqhgd', p.astype(v.dtype), v)

    o = lax.map(one, qb)
    return jnp.moveaxis(o, 0, 1).reshape(B, L, HK * G * d)


def window_attention_latent(q, k, v, kc, vc, sink):
    B, L, HK, G, d = q.shape
    Lc = kc.shape[1]
    nb = L // WIN_BLOCK
    qb = q.reshape(B, nb, WIN_BLOCK, HK, G, d)

    def bands(t):
        tp = jnp.pad(t, ((0, 0), (WIN_BLOCK, WIN_BLOCK), (0, 0), (0, 0)))
        return jnp.concatenate(
            [tp[:, i * WIN_BLOCK:i * WIN_BLOCK + L].reshape(B, nb, WIN_BLOCK, HK, d) for i in range(3)], axis=2)

    kb, vb = bands(k), bands(v)
    qpos = np.arange(nb)[:, None] * WIN_BLOCK + np.arange(WIN_BLOCK)[None, :]
    kpos = np.arange(nb)[:, None] * WIN_BLOCK - WIN_BLOCK + np.arange(3 * WIN_BLOCK)[None, :]
    valid = ((np.abs(qpos[:, :, None] - kpos[:, None, :]) <= WINDOW)
             & (kpos[:, None, :] >= 0) & (kpos[:, None, :] < L))
    s_loc = jnp.einsum('bnqhgd,bnkhd->bnhgqk', qb, kb).astype(jnp.float32) * ATT_SCALE
    s_loc = jnp.where(valid[None, :, None, None], s_loc, NEG_INF)
    s_ctx = jnp.einsum('bnqhgd,bchd->bnhgqc', qb, kc).astype(jnp.float32) * ATT_SCALE
    col = jnp.broadcast_to(sink.astype(jnp.float32)[None, None, :, :, None, None], s_loc.shape[:-1] + (1,))
    p = jax.nn.softmax(jnp.concatenate([s_loc, s_ctx, col], axis=-1), axis=-1)
    nl = 3 * WIN_BLOCK
    p_loc = p[..., :nl].astype(v.dtype)
    p_ctx = p[..., nl:nl + Lc].astype(v.dtype)
    o = jnp.einsum('bnhgqk,bnkhd->bnqhgd', p_loc, vb) + jnp.einsum('bnhgqc,bchd->bnqhgd', p_ctx, vc)
    return o.reshape(B, L, HK * G * d)


def na_latent(q, k, v, kc, vc, rpb):
    B, L, H, d = q.shape
    R = L // GRID_W
    WR = min(NA_ROWS, R)
    NCB = GRID_W // NA_QBLK
    r = np.arange(R)
    row_start = np.clip(r - WR // 2, 0, R - WR)
    row_idx = row_start[:, None] + np.arange(WR)[None, :]
    j = np.arange(NCB)
    cb_start = np.clip(j * NA_QBLK - NA_COLS // 2, 0, GRID_W - NA_KBLK)
    col_idx = cb_start[:, None] + np.arange(NA_KBLK)[None, :]
    qcol = j[:, None] * NA_QBLK + np.arange(NA_QBLK)[None, :]
    qcol_start = np.clip(qcol - NA_COLS // 2, 0, GRID_W - NA_COLS)
    kcol = col_idx[:, None, :]
    col_valid = (kcol >= qcol_start[:, :, None]) & (kcol < qcol_start[:, :, None] + NA_COLS)
    dr_idx = row_idx - r[:, None] + NA_ROWS - 1
    dc_idx = np.clip(kcol - qcol[:, :, None] + NA_COLS - 1, 0, 2 * NA_COLS - 2)

    krb = k.reshape(B, R, GRID_W, H, d)[:, row_idx][:, :, :, col_idx]
    vrb = v.reshape(B, R, GRID_W, H, d)[:, row_idx][:, :, :, col_idx]
    qg = q.reshape(B, R, NCB, NA_QBLK, H, d)
    s_loc = jnp.einsum('brjqhd,brwjkhd->brjhqwk', qg, krb).astype(jnp.float32) * ATT_SCALE
    bias = rpb.astype(jnp.float32)[:, dr_idx[:, None, None, :, None], dc_idx[None, :, :, None, :]]
    bias = jnp.transpose(bias, (1, 2, 0, 3, 4, 5))
    s_loc = jnp.where(col_valid[None, None, :, None, :, None, :], s_loc + bias[None], NEG_INF)
    nl = WR * NA_KBLK
    s_loc = s_loc.reshape(B, R, NCB, H, NA_QBLK, nl)
    s_ctx = jnp.einsum('brjqhd,bchd->brjhqc', qg, kc).astype(jnp.float32) * ATT_SCALE
    p = jax.nn.softmax(jnp.concatenate([s_loc, s_ctx], axis=-1), axis=-1)
    p_loc = p[..., :nl].reshape(B, R, NCB, H, NA_QBLK, WR, NA_KBLK).astype(v.dtype)
    p_ctx = p[..., nl:].astype(v.dtype)
    o = jnp.einsum('brjhqwk,brwjkhd->brjqhd', p_loc, vrb) + jnp.einsum('brjhqc,bchd->brjqhd', p_ctx, vc)
    return o.reshape(B, L, H * d)


def merge_branches(o_ssm, o_win, o_na, ga, gb, gc, w_branch, w_out):
    m = (jax.nn.sigmoid(ga) * (o_ssm @ w_branch[0])
         + jax.nn.sigmoid(gb) * (o_win @ w_branch[1])
         + jax.nn.sigmoid(gc) * (o_na @ w_branch[2]))
    return m @ w_out


def conv_ffn(h, w_up, conv_w, conv_b, w_down):
    L = h.shape[1]
    u = h @ w_up
    pad = CONV_W // 2
    up = jnp.pad(u, ((0, 0), (pad, pad), (0, 0)))
    u = sum(up[:, i:i + L] * conv_w[i] for i in range(CONV_W)) + conv_b
    a, b = jnp.split(u, 2, axis=-1)
    return (jax.nn.silu(a) * b) @ w_down


def setup_inputs(seed: int = 0) -> dict:
    key = jax.random.key(seed)
    ks = iter(jax.random.split(key, 48))
    f32 = jnp.float32

    def nrm(shape, scale):
        return jax.random.normal(next(ks), shape, f32) * scale

    n = jnp.arange(SSM_STATE, dtype=f32)
    return {
        'x_prompt': nrm((BATCH, SEQ, D_MODEL), 1.0),
        'x_sample': nrm((DEC_BATCH, DEC_SEQ, D_MODEL), 1.0),
        'cache_win_k': nrm((DEC_BATCH, DEPTH, PAST_LEN, WIN_KV_HEADS, HEAD_DIM), 1.0),
        'cache_win_v': nrm((DEC_BATCH, DEPTH, PAST_LEN, WIN_KV_HEADS, HEAD_DIM), 1.0),
        'cache_na_k': nrm((DEC_BATCH, DEPTH, PAST_LEN, NA_HEADS, HEAD_DIM), 1.0),
        'cache_na_v': nrm((DEC_BATCH, DEPTH, PAST_LEN, NA_HEADS, HEAD_DIM), 1.0),
        'state_ssm_re': nrm((DEC_BATCH, DEPTH, 2, SSM_GROUPS, SSM_STATE), 0.1),
        'state_ssm_im': nrm((DEC_BATCH, DEPTH, 2, SSM_GROUPS, SSM_STATE), 0.1),
        'c': nrm((DEC_BATCH, D_MODEL), 1.0),
        'c_ctx': nrm((D_MODEL,), 1.0),
        'norm1_g': 1.0 + nrm((DEPTH, D_MODEL), 0.02),
        'norm2_g': 1.0 + nrm((DEPTH, D_MODEL), 0.02),
        'w_mod': nrm((DEPTH, D_MODEL, 6 * D_MODEL), D_MODEL ** -0.5),
        'b_mod': nrm((DEPTH, 6 * D_MODEL), 0.01),
        'w_in': nrm((DEPTH, D_MODEL, N_IN), D_MODEL ** -0.5),
        'ssm_lam_re': -0.5 + nrm((DEPTH, 2, SSM_GROUPS, SSM_STATE), 0.01),
        'ssm_lam_im': jnp.pi * n + nrm((DEPTH, 2, SSM_GROUPS, SSM_STATE), 0.01),
        'ssm_log_dt': jax.random.uniform(next(ks), (DEPTH, 2, SSM_GROUPS), f32, math.log(1e-3), math.log(1e-1)),
        'ssm_b_re': nrm((DEPTH, 2, SSM_GROUPS, SSM_STATE, SSM_GROUP), (2 * SSM_GROUP) ** -0.5),
        'ssm_b_im': nrm((DEPTH, 2, SSM_GROUPS, SSM_STATE, SSM_GROUP), (2 * SSM_GROUP) ** -0.5),
        'ssm_c_re': nrm((DEPTH, 2, SSM_GROUPS, SSM_GROUP, SSM_STATE), SSM_STATE ** -0.5),
        'ssm_c_im': nrm((DEPTH, 2, SSM_GROUPS, SSM_GROUP, SSM_STATE), SSM_STATE ** -0.5),
        'ssm_d': nrm((DEPTH, SSM_GROUPS, SSM_GROUP), 1.0),
        'w_glu': nrm((DEPTH, SSM_WIDTH, SSM_WIDTH), SSM_WIDTH ** -0.5),
        'b_glu': nrm((DEPTH, SSM_WIDTH), 0.01),
        'win_sink': nrm((DEPTH, WIN_HEADS), 0.5),
        'na_rpb': nrm((DEPTH, NA_HEADS, 2 * NA_ROWS - 1, 2 * NA_COLS - 1), 0.1),
        'w_branch': nrm((DEPTH, N_BRANCH, BRANCH_W, D_MODEL), BRANCH_W ** -0.5),
        'w_out': nrm((DEPTH, D_MODEL, D_MODEL), D_MODEL ** -0.5),
        'w_up': nrm((DEPTH, D_MODEL, 2 * D_FF), D_MODEL ** -0.5),
        'conv_w': nrm((DEPTH, CONV_W, 2 * D_FF), 0.5),
        'conv_b': nrm((DEPTH, 2 * D_FF), 0.01),
        'w_down': nrm((DEPTH, D_FF, D_MODEL), D_FF ** -0.5),
        'final_g': 1.0 + nrm((D_MODEL,), 0.02),
    }


def reference(x_prompt, x_sample, cache_win_k, cache_win_v, cache_na_k, cache_na_v, state_ssm_re, state_ssm_im,
              c, c_ctx, norm1_g, norm2_g, w_mod, b_mod, w_in, ssm_lam_re, ssm_lam_im, ssm_log_dt,
              ssm_b_re, ssm_b_im, ssm_c_re, ssm_c_im, ssm_d, w_glu, b_glu, win_sink, na_rpb,
              w_branch, w_out, w_up, conv_w, conv_b, w_down, final_g):
    xp, xs = x_prompt, x_sample
    Bp, Lp, _ = xp.shape
    Bs, Ls, _ = xs.shape
    new_wk, new_wv, new_nk, new_nv, new_sre, new_sim = [], [], [], [], [], []
    for l in range(DEPTH):
        ssm_p = (ssm_lam_re[l], ssm_lam_im[l], ssm_log_dt[l], ssm_b_re[l], ssm_b_im[l],
                 ssm_c_re[l], ssm_c_im[l], ssm_d[l], w_glu[l], b_glu[l])
        sink = win_sink[l].reshape(WIN_KV_HEADS, WIN_GROUP)

        sh1, sc1, g1, sh2, sc2, g2 = adaln(c_ctx, w_mod[l], b_mod[l])
        h = modulate(rmsnorm(xp, norm1_g[l]), sh1, sc1)
        u, qw, kw, vw, qn, kn, vn, ga, gb, gc = split_in(h @ w_in[l])
        kw, vw = heads(kw, WIN_KV_HEADS), heads(vw, WIN_KV_HEADS)
        kn, vn = heads(kn, NA_HEADS), heads(vn, NA_HEADS)
        o_ssm, xf, xb = ssm_mixer(u, *ssm_p, None, None)
        fin = jnp.stack([xf[:, -1], xb[:, 0]], axis=1)
        o_win = ctx_self_attention(qw.reshape(Bp, Lp, WIN_KV_HEADS, WIN_GROUP, HEAD_DIM), kw, vw, sink)
        o_na = ctx_self_attention(qn.reshape(Bp, Lp, NA_HEADS, 1, HEAD_DIM), kn, vn, None)
        xp = xp + g1 * merge_branches(o_ssm, o_win, o_na, ga, gb, gc, w_branch[l], w_out[l])
        h = modulate(rmsnorm(xp, norm2_g[l]), sh2, sc2)
        xp = xp + g2 * conv_ffn(h, w_up[l], conv_w[l], conv_b[l], w_down[l])
        new_wk.append(kw)
        new_wv.append(vw)
        new_nk.append(kn)
        new_nv.append(vn)
        new_sre.append(jnp.real(fin))
        new_sim.append(jnp.imag(fin))

        sh1, sc1, g1, sh2, sc2, g2 = [t[:, None, :] for t in adaln(c, w_mod[l], b_mod[l])]
        h = modulate(rmsnorm(xs, norm1_g[l]), sh1, sc1)
        u, qw, kw, vw, qn, kn, vn, ga, gb, gc = split_in(h @ w_in[l])
        o_ssm, _, _ = ssm_mixer(u, *ssm_p, state_ssm_re[:, l], state_ssm_im[:, l])
        qw = rope_2d(heads(qw, WIN_HEADS)).reshape(Bs, Ls, WIN_KV_HEADS, WIN_GROUP, HEAD_DIM)
        kw = rope_2d(heads(kw, WIN_KV_HEADS))
        o_win = window_attention_latent(qw, kw, heads(vw, WIN_KV_HEADS), cache_win_k[:, l], cache_win_v[:, l], sink)
        o_na = na_latent(heads(qn, NA_HEADS), heads(kn, NA_HEADS), heads(vn, NA_HEADS),
                         cache_na_k[:, l], cache_na_v[:, l], na_rpb[l])
        xs = xs + g1 * merge_branches(o_ssm, o_win, o_na, ga, gb, gc, w_branch[l], w_out[l])
        h = modulate(rmsnorm(xs, norm2_g[l]), sh2, sc2)
        xs = xs + g2 * conv_ffn(h, w_up[l], conv_w[l], conv_b[l], w_down[l])

    y_prompt = rmsnorm(xp, final_g)
    y_sample = rmsnorm(xs, final_g)
    new_win_k = jnp.stack(new_wk, axis=1)
    new_win_v = jnp.stack(new_wv, axis=1)
    new_na_k = jnp.stack(new_nk, axis=1)
    new_na_v = jnp.stack(new_nv, axis=1)
    new_ssm_re = jnp.stack(new_sre, axis=1)
    new_ssm_im = jnp.stack(new_sim, axis=1)
    return (y_prompt, y_sample, new_win_k, new_win_v, new_na_k, new_na_v, new_ssm_re, new_ssm_im)
```

```python
import contextlib
import math
import numpy as np
import concourse.bass as bass
import concourse.mybir as mybir
from concourse.ap import AP
from concourse.bass_utils import run_bass_kernel_spmd

F32 = mybir.dt.float32
BF16 = mybir.dt.bfloat16
I32 = mybir.dt.int32
ALU = mybir.AluOpType
AF = mybir.ActivationFunctionType

D = 2048
T = 1024
DEPTH = 4
STOP = 9
NIN = 10496
DFF = 5632
OFF_U, OFF_QW, OFF_KW, OFF_VW, OFF_QN, OFF_KN, OFF_VN, OFF_G = 0, 768, 1536, 1792, 2048, 2816, 3584, 4352
NEG = -30000.0
TWO_PI = 2.0 * math.pi
NA_KB = [list(range(0, 4)), list(range(0, 4)), list(range(0, 5)), list(range(1, 6)),
         list(range(2, 7)), list(range(3, 8)), list(range(4, 8)), list(range(4, 8))]
NA_IDX = {}
for _j in range(8):
    for _b in NA_KB[_j]:
        NA_IDX[(_j, _b)] = len(NA_IDX)
N_NAM = len(NA_IDX)


class Buf:
    __slots__ = ("name", "w", "r", "excl")

    def __init__(self, name="", excl=False):
        self.name = name
        self.w = None
        self.r = {}
        self.excl = excl


class Sched:
    ENG = ("pe", "act", "dve", "pool", "sp")

    def __init__(self, nc, stack, n_dma_sems=32):
        self.nc = nc
        self.e = {"pe": nc.tensor, "act": nc.scalar, "dve": nc.vector, "pool": nc.gpsimd, "sp": nc.sync}
        self.sem = {k: stack.enter_context(nc.semaphore("s_" + k)) for k in self.ENG}
        self.cnt = {k: 0 for k in self.ENG}
        self.seen = {k: {} for k in self.ENG}
        self.dsem = [stack.enter_context(nc.semaphore("d%d" % i)) for i in range(n_dma_sems)]
        self.dcnt = [0] * n_dma_sems
        self.dnext = 0
        self.n_ins = 0
        self.pe_pending = False
        self.inflight = {}
        self.max_inflight = 2

    def _wait(self, eng, kind, key, val):
        if kind == "e" and key == "pe" and eng == "pe":
            return
        if kind == "e" and key == "pe" and val > self.cnt["pe"]:
            assert self.pe_pending and val == self.cnt["pe"] + 1
            self.last_pe.then_inc(self.sem["pe"], 1)
            self.cnt["pe"] += 1
            self.pe_pending = False
        k = (kind, key)
        if self.seen[eng].get(k, 0) >= val:
            return
        sem = self.sem[key] if kind == "e" else self.dsem[key]
        self.e[eng].wait_ge(sem, val)
        self.seen[eng][k] = val

    def _deps(self, eng, reads, writes):
        best = {}
        for b in reads:
            if b.w is not None:
                k = (b.w[0], b.w[1])
                if best.get(k, 0) < b.w[2]:
                    best[k] = b.w[2]
        for b in writes:
            if b.w is not None:
                k = (b.w[0], b.w[1])
                if best.get(k, 0) < b.w[2]:
                    best[k] = b.w[2]
            for k, v in b.r.items():
                if best.get(k, 0) < v:
                    best[k] = v
        for (kind, key), val in best.items():
            self._wait(eng, kind, key, val)

    def _commit(self, tok, reads, writes):
        k = (tok[0], tok[1])
        for b in reads:
            if b.r.get(k, 0) < tok[2]:
                b.r[k] = tok[2]
        for b in writes:
            b.w = tok
            b.r = {}

    def op(self, eng, fn, reads=(), writes=(), inc=True):
        if any(b.excl for b in reads):
            writes = list(writes) + [b for b in reads if b.excl]
            reads = [b for b in reads if not b.excl]
        self._deps(eng, reads, writes)
        ins = fn(self.e[eng])
        if inc:
            self.cnt[eng] += 1
            ins.then_inc(self.sem[eng], 1)
            tok = ("e", eng, self.cnt[eng])
            if eng == "pe":
                self.pe_pending = False
        else:
            assert eng == "pe"
            tok = ("e", eng, self.cnt[eng] + 1)
            self.pe_pending = True
            self.last_pe = ins
        self._commit(tok, reads, writes)
        self.n_ins += 1
        return ins

    def dma(self, eng, out, in_, reads=(), writes=(), **kw):
        i = self.dnext
        self.dnext = (self.dnext + 1) % len(self.dsem)
        if self.dcnt[i] > 0:
            self._wait(eng, "d", i, self.dcnt[i])
        self._deps(eng, reads, writes)
        q = self.inflight.setdefault(eng, [])
        while len(q) >= self.max_inflight:
            t = q.pop(0)
            self._wait(eng, t[0], t[1], t[2])
        ins = self.e[eng].dma_start(out=out, in_=in_, **kw)
        self.dcnt[i] += 16
        ins.then_inc(self.dsem[i], 16)
        tok = ("d", i, self.dcnt[i])
        q.append(tok)
        self._commit(tok, reads, writes)
        self.n_ins += 1
        return tok

    def barrier(self):
        assert not self.pe_pending
        for eng in self.ENG:
            for other in self.ENG:
                if other != eng and self.cnt[other] > 0:
                    self._wait(eng, "e", other, self.cnt[other])
            for i, c in enumerate(self.dcnt):
                if c > 0:
                    self._wait(eng, "d", i, c)


def build_nc(depth=DEPTH):
    nc = bass.Bass("TRN2", target_bir_lowering=False)

    def din(name, shape):
        return nc.dram_tensor(name, list(shape), F32, kind="ExternalInput").ap()

    def dout(name, shape):
        return nc.dram_tensor(name, list(shape), F32, kind="ExternalOutput").ap()

    d_xT = din("xT", [D, T])
    d_cond = din("cond", [128, 16])
    d_flags = din("flags", [128, 4])
    d_h0 = din("h0", [128, depth * 2 * 24 * 2])
    d_kwc = din("kwc", [depth, 512, 512])
    d_knc = din("knc", [depth, 768, 512])
    d_vwc = din("vwc", [depth, 512, 256])
    d_vnc = din("vnc", [depth, 512, 768])
    d_wmask = din("wmask", [128, 24 * 128])
    d_nmask = din("nmask", [128, N_NAM * 128])
    d_nar = din("nar", [depth, 12, 128, 7 * 128])
    d_rope = din("rope", [128, 2 * T])
    d_consts = din("consts", [128, 128 * 3 + 256 + T])
    d_wmod = din("w_mod", [depth, D, 6 * D])
    d_bmod = din("b_mod", [128, depth * 96])
    d_n1g = din("n1g", [128, depth * 16])
    d_n2g = din("n2g", [128, depth * 16])
    d_fg = din("fg", [128, 16])
    d_win = din("w_in", [depth, D, NIN])
    d_wbr = din("w_branch", [depth, 3, 768, D])
    d_wout = din("w_out", [depth, D, D])
    d_wup = din("w_up", [depth, D, 2 * DFF])
    d_wdn = din("w_down", [depth, DFF, D])
    d_wglu = din("w_glu", [depth, 768, 768])
    d_bglu = din("b_glu", [128, depth * 6])
    d_convw = din("conv_w", [128, depth * 3 * 88])
    d_convb = din("conv_b", [128, depth * 88])
    d_ssmd = din("ssm_d", [128, depth * 6])
    d_sink = din("sink", [128, depth * 12])
    d_lamre = din("lam_re", [128, depth * 48])
    d_lamim = din("lam_im", [128, depth * 48])
    d_logdt = din("log_dt", [128, depth * 48])
    d_bmat = din("bmat", [depth, 24, 128, 4 * 128])
    d_cmat = din("cmat", [depth, 24, 128, 4 * 128])
    o_yT = dout("yT", [D, T])
    o_kv = dout("kv", [depth, T, 2048])
    o_fin = dout("fin", [128, depth * 4 * 2 * 24 * 2])

    st = contextlib.ExitStack()
    with st:
        S = Sched(nc, st)

        uid = [0]

        def sb(name, shape, dt, stack=st):
            uid[0] += 1
            return stack.enter_context(nc.sbuf_tensor("%s_%d" % (name, uid[0]), list(shape), dt))

        XT = sb("XT", [128, 16, T], F32)
        bXT = [Buf("XT%d" % i) for i in range(16)]
        HT = sb("HT", [128, 16, T], BF16)
        bHT = [Buf("HT%d" % i) for i in range(16)]
        NSLOT = 2
        WSL = [sb("wsl%d" % i, [128, 4096], BF16) for i in range(NSLOT)]
        bWSL = [Buf("wsl%d" % i) for i in range(NSLOT)]
        wnext = [0]
        ident = sb("ident", [128, 128], BF16)
        pswap = sb("pswap", [128, 128], BF16)
        ones_bf = sb("ones_bf", [128, 128], BF16)
        small = sb("small", [128, 1152], F32)
        bC = Buf("consts")
        bSmall = Buf("small")
        PS = [st.enter_context(nc.psum_tensor("ps%d" % i, [128, 1024], F32)) for i in range(4)]
        bPS = [[Buf("ps%d_%d" % (i, h), excl=True) for h in range(2)] for i in range(4)]

        col = {}
        cpos = [0]

        def scol(name, n):
            col[name] = cpos[0]
            cpos[0] += n
            assert cpos[0] <= 1152
            return col[name]

        for nm, n in [("cond", 16), ("flags", 4), ("bmod", depth * 96), ("n1g", depth * 16), ("n2g", depth * 16),
                      ("fg", 16), ("bglu", depth * 6), ("ssmd", depth * 6), ("sink", depth * 12)]:
            scol(nm, n)
        for nm in ["modv0", "modv1", "A10", "A11", "A20", "A21"]:
            scol(nm, 96 if nm.startswith("modv") else 16)
        scol("nw0", 88), scol("nw2", 88), scol("sinke", 12), scol("mhpi", 1)

        def SM(name, a=0, n=1):
            return small[:, col[name] + a: col[name] + a + n]

        big2 = sb("big2", [128, depth * 3 * 88 + depth * 88], F32)
        CW0 = 0
        CB0 = depth * 3 * 88
        ssmp = sb("ssmp", [128, 3 * depth * 48 + 2 * depth * 96], F32)
        fin = sb("fin", [128, 4 * 2 * 24 * 2], F32)
        bFin = Buf("fin")

        def ld(dst, src, eng="sp"):
            S.dma(eng, dst, src, writes=[bC])

        ld(small[:, col["cond"]:col["cond"] + 16], d_cond[:, :])
        ld(small[:, col["flags"]:col["flags"] + 4], d_flags[:, :])
        ld(small[:, col["bmod"]:col["bmod"] + depth * 96], d_bmod[:, :])
        ld(small[:, col["n1g"]:col["n1g"] + depth * 16], d_n1g[:, :])
        ld(small[:, col["n2g"]:col["n2g"] + depth * 16], d_n2g[:, :])
        ld(small[:, col["fg"]:col["fg"] + 16], d_fg[:, :])
        ld(small[:, col["bglu"]:col["bglu"] + depth * 6], d_bglu[:, :])
        ld(small[:, col["ssmd"]:col["ssmd"] + depth * 6], d_ssmd[:, :])
        ld(small[:, col["sink"]:col["sink"] + depth * 12], d_sink[:, :])
        ld(big2[:, CW0:CW0 + depth * 264], d_convw[:, :])
        ld(big2[:, CB0:CB0 + depth * 88], d_convb[:, :])
        ld(ssmp[:, 0:depth * 48], d_lamre[:, :])
        ld(ssmp[:, depth * 48:2 * depth * 48], d_lamim[:, :])
        ld(ssmp[:, 2 * depth * 48:3 * depth * 48], d_logdt[:, :])
        ld(ssmp[:, 3 * depth * 48:3 * depth * 48 + depth * 96], d_h0[:, :])
        ld(ident[:], d_consts[:, 0:128], "pool")
        ld(pswap[:], d_consts[:, 128:256], "pool")
        ld(ones_bf[:], d_consts[:, 256:384], "pool")
        for kt in range(16):
            S.dma("sp", XT[:, kt, :], d_xT[kt * 128:(kt + 1) * 128, :], writes=[bXT[kt]])
        S.op("pool", lambda e: e.memset(small[:, col["mhpi"]:col["mhpi"] + 1], -math.pi / 2), writes=[bC])
        S.barrier()

        def wslot():
            i = wnext[0] % len(WSL)
            wnext[0] = (i + 1) % len(WSL)
            return WSL[i], bWSL[i]

        @contextlib.contextmanager
        def extra_slots(stk, n):
            base = len(WSL)
            for i in range(n):
                WSL.append(sb("wslx%d" % i, [128, 4096], BF16, stk))
                bWSL.append(Buf("wslx%d" % i))
            try:
                yield
            finally:
                del WSL[base:]
                del bWSL[base:]
                wnext[0] = 0

        def load_w(segs, nk, width):
            slot, b = wslot()
            view = slot[:, 0:nk * width].rearrange("p (k c) -> p k c", k=nk)
            for src, c0 in segs:
                w = src.shape[1]
                S.dma("pool", view[:, :, c0:c0 + w], src.rearrange("(k p) c -> p k c", p=128), writes=[b])
            return view, b

        def mm(out, lhsT, rhs, start, stop, reads, writes, inc=None):
            if inc is None:
                inc = stop
            S.op("pe", lambda e: e.matmul(out, lhsT=lhsT, rhs=rhs, start=start, stop=stop),
                 reads=reads, writes=writes, inc=inc)

        def psh(i, h):
            return PS[i][:, h * 512:(h + 1) * 512]

        def rev(ap2d):
            (ps_, pc), (fs, fc) = ap2d.ap
            last = ap2d[:, fc - 1:fc]
            return AP(ap2d.tensor, last.offset, [[ps_, pc], [-fs, fc]])

        def chunked(ap2d, n, reverse=False):
            (ps_, pc), (fs, fc) = ap2d.ap
            if reverse:
                last = ap2d[:, fc - 1:fc]
                return AP(ap2d.tensor, last.offset, [[ps_, pc], [0, n], [-fs, fc]])
            return AP(ap2d.tensor, ap2d.offset, [[ps_, pc], [0, n], [fs, fc]])

        scb = sb("scb", [128, 16], BF16)
        bScb = Buf("scb")
        S.op("act", lambda e: e.activation(out=scb[:], in_=SM("cond", 0, 16), func=AF.Silu), reads=[bC], writes=[bScb])

        def phase_mod_gen(l):
            pb = bPS[3][0]
            mv, a1, a2 = "modv%d" % (l % 2), "A1%d" % (l % 2), "A2%d" % (l % 2)
            for pi in range(48):
                wp, wb = load_w([(d_wmod[l][:, pi * 256:(pi + 1) * 256], 0)], 16, 256)
                for j in range(2):
                    mt = pi * 2 + j
                    for kt in range(16):
                        mm(PS[3][:, mt:mt + 1], wp[:, kt, j * 128:(j + 1) * 128], scb[:, kt:kt + 1],
                           kt == 0, kt == 15, [wb, bScb], [pb])
                yield
            S.op("dve", lambda e: e.tensor_tensor(out=SM(mv, 0, 96), in0=PS[3][:, 0:96],
                                                  in1=SM("bmod", l * 96, 96), op=ALU.add),
                 reads=[pb, bC], writes=[bSmall])
            S.op("dve", lambda e: e.scalar_tensor_tensor(out=SM(a1, 0, 16), in0=SM(mv, 16, 16), scalar=1.0,
                                                         in1=SM("n1g", l * 16, 16), op0=ALU.add, op1=ALU.mult),
                 reads=[bSmall, bC], writes=[bSmall])
            S.op("dve", lambda e: e.scalar_tensor_tensor(out=SM(a2, 0, 16), in0=SM(mv, 64, 16), scalar=1.0,
                                                         in1=SM("n2g", l * 16, 16), op0=ALU.add, op1=ALU.mult),
                 reads=[bSmall, bC], writes=[bSmall])
            yield

        def phase_mod(l):
            for _ in phase_mod_gen(l):
                pass

        def phase_norm(Aname, Aoff, Bname, Boff, stk, final=False):
            sq = [sb("sq%d" % i, [128, T], BF16, stk) for i in range(2)]
            bsq = [Buf(), Buf()]
            rstd = sb("rstd", [128, T], F32, stk)
            brs = Buf()
            tmp = [sb("ntmp%d" % i, [128, T], F32, stk) for i in range(2)]
            btmp = [Buf(), Buf()]
            for kt in range(16):
                i = kt % 2
                S.op("act", lambda e: e.activation(out=sq[i][:], in_=XT[:, kt, :], func=AF.Square),
                     reads=[bXT[kt]], writes=[bsq[i]])
                for hf in range(2):
                    mm(psh(3, hf), ones_bf[:], sq[i][:, hf * 512:(hf + 1) * 512], kt == 0, kt == 15,
                       [bsq[i], bC], [bPS[3][hf]])
            S.op("dve", lambda e: e.tensor_scalar(out=rstd[:], in0=PS[3][:, :], scalar1=1.0 / D, scalar2=1e-6,
                                                  op0=ALU.mult, op1=ALU.add), reads=bPS[3], writes=[brs])
            S.op("act", lambda e: e.activation(out=rstd[:], in_=rstd[:], func=AF.Sqrt), reads=[brs], writes=[brs])
            S.op("dve", lambda e: e.reciprocal(out=rstd[:], in_=rstd[:]), reads=[brs], writes=[brs])
            for kt in range(16):
                i = kt % 2
                S.op("dve", lambda e: e.scalar_tensor_tensor(out=tmp[i][:], in0=XT[:, kt, :],
                                                             scalar=SM(Aname, Aoff + kt, 1), in1=rstd[:],
                                                             op0=ALU.mult, op1=ALU.mult),
                     reads=[bXT[kt], brs, bSmall, bC], writes=[btmp[i]])
                if final:
                    S.dma("sp", o_yT[kt * 128:(kt + 1) * 128, :], tmp[i][:], reads=[btmp[i]], writes=[bOut])
                else:
                    S.op("act", lambda e: e.activation(out=HT[:, kt, :], in_=tmp[i][:], func=AF.Identity,
                                                       bias=SM(Bname, Boff + kt, 1)),
                         reads=[btmp[i], bSmall], writes=[bHT[kt]])

        bOut = Buf("out")

        pcount = [0]

        def proj_fm(l, colsegs_per_mtile, evac):
            n = len(colsegs_per_mtile)
            for m0 in range(0, n, 2):
                grp = colsegs_per_mtile[m0:m0 + 2]
                segs = []
                c = 0
                for cs in grp:
                    for (c0, w) in cs:
                        segs.append((d_win[l][:, c0:c0 + w], c))
                        c += w
                wp, wb = load_w(segs, 16, 128 * len(grp))
                for j in range(len(grp)):
                    for hf in range(2):
                        bank = pcount[0] % 2
                        pcount[0] += 1
                        for kt in range(16):
                            mm(psh(bank, hf), wp[:, kt, j * 128:(j + 1) * 128], HT[:, kt, hf * 512:(hf + 1) * 512],
                               kt == 0, kt == 15, [wb, bHT[kt]], [bPS[bank][hf]])
                        evac(m0 + j, hf, psh(bank, hf), bPS[bank][hf])

        def tokmajor(l, c0, w, kvcol, vdst, stgs):
            stg, bst = stgs
            wp, wb = load_w([(d_win[l][:, c0:c0 + w], 0)], 16, w)
            for tb in range(8):
                bank = pcount[0] % 2
                pcount[0] += 1
                for kt in range(16):
                    mm(PS[bank][:, 0:w], HT[:, kt, tb * 128:(tb + 1) * 128], wp[:, kt, :], kt == 0, kt == 15,
                       [wb, bHT[kt]], [bPS[bank][0]])
                i = tb % 2
                S.op("act", lambda e: e.activation(out=stg[i][:, 0:w], in_=PS[bank][:, 0:w], func=AF.Copy),
                     reads=[bPS[bank][0]], writes=[bst[i]])
                S.dma("sp", o_kv[l][tb * 128:(tb + 1) * 128, kvcol:kvcol + w], stg[i][:, 0:w], reads=[bst[i]],
                      writes=[bOut])
                if vdst is not None:
                    vt, vb, voff = vdst
                    S.op("dve", lambda e: e.tensor_copy(out=vt[:, tb, voff:voff + w], in_=PS[bank][:, 0:w]),
                         reads=[bPS[bank][0]], writes=[vb])

        def attn_head(qtile, bq, qt, qh, ktile, bk, ktidx, kctx, bkc, kcidx, vaug_loc, vaug_ctx, bv, bvc,
                      local_units, sink_ap, stk_bufs):
            (pt, bpt, rinv, brinv) = stk_bufs
            pr = slice(qh * 64, qh * 64 + 64)
            bqo = Buf("qout")

            def stage_a(n):
                qs = slice(n * 128, (n + 1) * 128)
                sa, sb_ = ((2, 3), (0, 1))[n % 2]
                q_ap = qtile[pr, qt, qs]
                units = local_units(n)
                nl = len(units)
                for u, (b, biases) in enumerate(units):
                    hf = u // 4
                    o_ap = PS[sa][:, u * 128:(u + 1) * 128]
                    mm(o_ap, ktile[pr, ktidx, b * 128:(b + 1) * 128], q_ap, True, len(biases) == 0,
                       [bk, bq], [bPS[sa][hf]], inc=(len(biases) == 0))
                    for bi, bias_ap in enumerate(biases):
                        mm(o_ap, ident[:], bias_ap, False, bi == len(biases) - 1, [bC], [bPS[sa][hf]])
                for cb in range(4):
                    mm(PS[sb_][:, cb * 128:(cb + 1) * 128], kctx[pr, kcidx, cb * 128:(cb + 1) * 128], q_ap, True, True,
                       [bkc, bq], [bPS[sb_][0]])
                i = n % 2
                S.op("act", lambda e: e.activation(out=pt[i][:, 0:nl * 128], in_=PS[sa][:, 0:nl * 128], func=AF.Exp),
                     reads=bPS[sa], writes=[bpt[i]])
                S.op("act", lambda e: e.activation(out=pt[i][:, 640:1152], in_=PS[sb_][:, 0:512], func=AF.Exp,
                                                   bias=SM("flags", 2, 1)),
                     reads=[bPS[sb_][0], bC], writes=[bpt[i]])

            def stage_b(n):
                qs = slice(n * 128, (n + 1) * 128)
                sa, sb_ = ((2, 3), (0, 1))[n % 2]
                units = local_units(n)
                i = n % 2
                for u, (b, _) in enumerate(units):
                    mm(PS[sb_][0:64, 512:640], vaug_loc(b), pt[i][:, u * 128:(u + 1) * 128], u == 0, False, [bv, bpt[i]],
                       [bPS[sb_][1]], inc=False)
                    mm(PS[sb_][64:128, 512:640], ones_bf[:, 0:64], pt[i][:, u * 128:(u + 1) * 128], u == 0, False,
                       [bC, bpt[i]], [bPS[sb_][1]], inc=False)
                for cb in range(4):
                    mm(PS[sb_][0:64, 512:640], vaug_ctx(cb), pt[i][:, 640 + cb * 128:640 + (cb + 1) * 128], False, cb == 3,
                       [bvc, bpt[i]], [bPS[sb_][1]], inc=False)
                    mm(PS[sb_][64:128, 512:640], ones_bf[:, 0:64], pt[i][:, 640 + cb * 128:640 + (cb + 1) * 128], False, cb == 3,
                       [bC, bpt[i]], [bPS[sb_][1]], inc=(cb == 3))
                if sink_ap is not None:
                    S.op("dve", lambda e: e.tensor_scalar(out=rinv[i][64:128, :], in0=PS[sb_][64:128, 512:640],
                                                          scalar1=sink_ap[64:128, :], scalar2=None, op0=ALU.add),
                         reads=[bPS[sb_][1], bSmall], writes=[brinv[i]])
                    S.op("dve", lambda e: e.reciprocal(out=rinv[i][64:128, :], in_=rinv[i][64:128, :]),
                         reads=[brinv[i]], writes=[brinv[i]])
                else:
                    S.op("dve", lambda e: e.reciprocal(out=rinv[i][64:128, :], in_=PS[sb_][64:128, 512:640]),
                         reads=[bPS[sb_][1]], writes=[brinv[i]])
                S.op("dve", lambda e: e.tensor_tensor(out=qtile[pr, qt, qs], in0=PS[sb_][0:64, 512:640],
                                                      in1=rinv[i][64:128, :], op=ALU.mult),
                     reads=[bPS[sb_][1], brinv[i]], writes=[bqo])

            stage_a(0)
            for n in range(8):
                if n + 1 < 8:
                    stage_a(n + 1)
                stage_b(n)

        def vaug_ap(vt, base, ones_off):
            return vt[:, base:base + 64]

        def layer(l):
            MV, A1N, A2N = "modv%d" % (l % 2), "A1%d" % (l % 2), "A2%d" % (l % 2)
            if l == 0 or STOP < 3:
                phase_mod(l)
            with contextlib.ExitStack() as stk:
                phase_norm(A1N, 0, MV, 0, stk)
                S.barrier()
            if STOP <= 1:
                return
            with contextlib.ExitStack() as mx:
                OS = sb("OS", [128, 6, T], BF16, mx)
                OW = sb("OW", [128, 6, T], BF16, mx)
                ON = sb("ON", [128, 6, T], BF16, mx)
                bOS = [Buf() for _ in range(6)]
                bOW = [Buf() for _ in range(6)]
                bON = [Buf() for _ in range(6)]

                with contextlib.ExitStack() as sk:
                    def ev_u(mt, hf, ps, pb):
                        S.op("act", lambda e: e.activation(out=OS[:, mt, hf * 512:(hf + 1) * 512], in_=ps, func=AF.Copy),
                             reads=[pb], writes=[bOS[mt]])
                    proj_fm(l, [[(OFF_U + m * 128, 128)] for m in range(6)], ev_u)
                    if STOP >= 3:
                        ssm_branch(l, OS, bOS, ON, bON, sk)
                    S.barrier()
                with contextlib.ExitStack() as wk:
                    if STOP >= 4:
                        win_branch(l, OW, bOW, wk)
                    S.barrier()
                for hg in range(2):
                    with contextlib.ExitStack() as nk:
                        if STOP >= 5:
                            na_branch(l, hg, ON, bON, nk)
                        S.barrier()
                with contextlib.ExitStack() as gk:
                    if STOP >= 6:
                        with extra_slots(gk, 2):
                            merge(l, OS, bOS, OW, bOW, ON, bON, gk)
                            S.barrier()
                    S.barrier()
            if STOP <= 6:
                return
            with contextlib.ExitStack() as stk:
                phase_norm(A2N, 0, MV, 48, stk)
                S.barrier()
            with contextlib.ExitStack() as fk:
                with extra_slots(fk, 3):
                    ffn(l, fk)
                    S.barrier()

        def ssm_branch(l, OS, bOS, YS, bYS, sk):
            P48 = depth * 48
            lam_re = ssmp[:, l * 48:(l + 1) * 48]
            lam_im = ssmp[:, P48 + l * 48:P48 + (l + 1) * 48]
            logdt = ssmp[:, 2 * P48 + l * 48:2 * P48 + (l + 1) * 48]
            H0 = 3 * P48 + l * 96
            ttab = sb("ttab", [128, 256], F32, sk)
            rmask = sb("rmask", [128, 512], F32, sk)
            S.dma("sp", ttab[:], d_consts[:, 384:640], writes=[bC])
            S.dma("sp", rmask[:], d_consts[:, 640:640 + 512], writes=[bC])
            sp_ = sb("sp_", [128, 13 * 48], F32, sk)
            bsp = Buf()
            ki = sb("ssm_ki", [128, 256], I32, sk)

            def P(i):
                return sp_[:, i * 48:(i + 1) * 48]
            A_, NA_, TH, MAG, SN, CS, FR, FI, T0, T1, T2, DEN, THN = [P(i) for i in range(13)]

            def dv(fn, r=(), w=()):
                S.op("dve", fn, reads=list(r) + [bsp], writes=list(w) + [bsp])

            def sincos(th_ap, out_s, out_c, x_t, k_t):
                for shift, dst in ((0.0, out_s), (math.pi / 2, out_c)):
                    dv(lambda e: e.tensor_scalar(out=k_t, in0=th_ap, scalar1=shift, scalar2=1.0 / TWO_PI,
                                                 op0=ALU.add, op1=ALU.mult))
                    dv(lambda e: e.tensor_copy(out=x_t, in_=k_t))
                    dv(lambda e: e.scalar_tensor_tensor(out=x_t, in0=x_t, scalar=-TWO_PI, in1=th_ap,
                                                        op0=ALU.mult, op1=ALU.add))
                    dv(lambda e: e.tensor_scalar(out=x_t, in0=x_t, scalar1=shift, scalar2=math.pi,
                                                 op0=ALU.add, op1=ALU.min))
                    dv(lambda e: e.tensor_scalar(out=x_t, in0=x_t, scalar1=-math.pi, scalar2=None, op0=ALU.max))
                    S.op("act", lambda e: e.activation(out=dst, in_=x_t, func=AF.Sin), reads=[bsp], writes=[bsp])

            S.op("act", lambda e: e.activation(out=T0, in_=logdt, func=AF.Exp), reads=[bC], writes=[bsp])
            dv(lambda e: e.tensor_tensor(out=A_, in0=T0, in1=lam_re, op=ALU.mult), r=[bC])
            dv(lambda e: e.tensor_tensor(out=TH, in0=T0, in1=lam_im, op=ALU.mult), r=[bC])
            dv(lambda e: e.tensor_scalar(out=NA_, in0=A_, scalar1=-1.0, scalar2=None, op0=ALU.mult))
            dv(lambda e: e.tensor_scalar(out=THN, in0=TH, scalar1=1.0 / TWO_PI, scalar2=None, op0=ALU.mult))
            S.op("act", lambda e: e.activation(out=MAG, in_=A_, func=AF.Exp), reads=[bsp], writes=[bsp])
            sincos(TH, SN, CS, T1, ki[:, 0:48])
            dv(lambda e: e.tensor_tensor(out=T0, in0=MAG, in1=CS, op=ALU.mult))
            dv(lambda e: e.tensor_scalar(out=T0, in0=T0, scalar1=-1.0, scalar2=None, op0=ALU.add))
            dv(lambda e: e.tensor_tensor(out=T1, in0=MAG, in1=SN, op=ALU.mult))
            dv(lambda e: e.tensor_tensor(out=DEN, in0=lam_re, in1=lam_re, op=ALU.mult), r=[bC])
            dv(lambda e: e.tensor_tensor(out=T2, in0=lam_im, in1=lam_im, op=ALU.mult), r=[bC])
            dv(lambda e: e.tensor_tensor(out=DEN, in0=DEN, in1=T2, op=ALU.add))
            dv(lambda e: e.reciprocal(out=DEN, in_=DEN))
            dv(lambda e: e.tensor_tensor(out=FR, in0=T0, in1=lam_re, op=ALU.mult), r=[bC])
            dv(lambda e: e.tensor_tensor(out=T2, in0=T1, in1=lam_im, op=ALU.mult), r=[bC])
            dv(lambda e: e.tensor_tensor(out=FR, in0=FR, in1=T2, op=ALU.add))
            dv(lambda e: e.tensor_tensor(out=FR, in0=FR, in1=DEN, op=ALU.mult))
            dv(lambda e: e.tensor_tensor(out=FI, in0=T1, in1=lam_re, op=ALU.mult), r=[bC])
            dv(lambda e: e.tensor_tensor(out=T2, in0=T0, in1=lam_im, op=ALU.mult), r=[bC])
            dv(lambda e: e.tensor_tensor(out=FI, in0=FI, in1=T2, op=ALU.subtract))
            dv(lambda e: e.tensor_tensor(out=FI, in0=FI, in1=DEN, op=ALU.mult))

            tb_ = sb("ssm_tab", [128, 6 * 256], F32, sk)
            bt = Buf()
            EPr, EPi, EMr, EMi = [sb("ssm_E%d" % i, [128, 256], F32, sk) for i in range(4)]
            bE = Buf()
            W1, W2, ZR, ZI, DT, P1, P2 = [sb("ssm_w%d" % i, [128, 512], F32, sk) for i in range(7)]
            bW1, bW2, bZR, bZI, bDT, bP1, bP2 = [Buf() for _ in range(7)]
            XR = sb("ssm_xr", [128, 512], BF16, sk)
            XI = sb("ssm_xi", [128, 512], BF16, sk)
            bX = Buf()
            pc_ = sb("ssm_pc", [128, 32], F32, sk)
            bpc = Buf()
            ytmp = sb("ssm_yt", [128, 512], F32, sk)
            byt = Buf()
            bm_t = [sb("ssm_bm%d" % i, [128, 4 * 128], BF16, sk) for i in range(2)]
            cm_t = [sb("ssm_cm%d" % i, [128, 4 * 128], BF16, sk) for i in range(2)]
            bbm = [Buf(), Buf()]

            def TB(i):
                return tb_[:, i * 256:(i + 1) * 256]

            def v2(a):
                return a.rearrange("p (c t) -> p c t", c=2)

            modgen = phase_mod_gen(l + 1) if (l + 1 < depth) else iter(())
            for kt in range(6):
                for g4 in range(4):
                    gp = kt * 4 + g4
                    sl = gp % 2
                    next(modgen, None)
                    next(modgen, None)
                    S.dma("pool", bm_t[sl][:], d_bmat[l][gp], writes=[bbm[sl]])
                    S.dma("pool", cm_t[sl][:], d_cmat[l][gp], writes=[bbm[sl]])
                    for d_ in range(2):
                        c = d_ * 24 + gp
                        r_ = (d_ == 1)
                        S.op("act", lambda e: e.activation(out=TB(4), in_=ttab[:], func=AF.Exp, scale=A_[:, c:c + 1]),
                             reads=[bsp, bC], writes=[bt])
                        S.op("act", lambda e: e.activation(out=TB(5), in_=ttab[:], func=AF.Exp, scale=NA_[:, c:c + 1]),
                             reads=[bsp, bC], writes=[bt])
                        S.op("act", lambda e: e.activation(out=TB(0), in_=ttab[:], func=AF.Identity, scale=TH[:, c:c + 1]),
                             reads=[bsp, bC], writes=[bt])
                        S.op("dve", lambda e: e.tensor_scalar(out=ki[:], in0=ttab[:], scalar1=THN[:, c:c + 1],
                                                              scalar2=None, op0=ALU.mult), reads=[bsp, bC, bt], writes=[bt])
                        S.op("dve", lambda e: e.tensor_copy(out=TB(1), in_=ki[:]), reads=[bt], writes=[bt])
                        S.op("dve", lambda e: e.scalar_tensor_tensor(out=TB(1), in0=TB(1), scalar=-TWO_PI, in1=TB(0),
                                                                     op0=ALU.mult, op1=ALU.add), reads=[bt], writes=[bt])
                        S.op("dve", lambda e: e.tensor_scalar(out=TB(1), in0=TB(1), scalar1=math.pi, scalar2=-math.pi,
                                                              op0=ALU.min, op1=ALU.max), reads=[bt], writes=[bt])
                        S.op("act", lambda e: e.activation(out=TB(2), in_=TB(1), func=AF.Sin), reads=[bt], writes=[bt])
                        S.op("dve", lambda e: e.scalar_tensor_tensor(out=TB(1), in0=TB(1), scalar=-1.0, in1=TB(1),
                                                                     op0=ALU.mult, op1=ALU.max), reads=[bt], writes=[bt])
                        S.op("act", lambda e: e.activation(out=TB(3), in_=TB(1), func=AF.Sin, bias=SM("mhpi", 0, 1)),
                             reads=[bt, bC], writes=[bt])
                        S.op("dve", lambda e: e.scalar_tensor_tensor(out=EPr[:], in0=TB(4), scalar=-1.0, in1=TB(3),
                                                                     op0=ALU.mult, op1=ALU.mult), reads=[bt, bE], writes=[bE])
                        S.op("dve", lambda e: e.tensor_tensor(out=EPi[:], in0=TB(4), in1=TB(2), op=ALU.mult), reads=[bt, bE], writes=[bE])
                        S.op("dve", lambda e: e.scalar_tensor_tensor(out=TB(4), in0=TB(5), scalar=-1.0, in1=TB(3),
                                                                     op0=ALU.mult, op1=ALU.mult), reads=[bt], writes=[bt])
                        S.op("dve", lambda e: e.tensor_tensor(out=TB(5), in0=TB(5), in1=TB(2), op=ALU.mult), reads=[bt], writes=[bt])
                        S.op("act", lambda e: e.activation(out=EMr[:], in_=TB(4), func=AF.Identity, scale=FR[:, c:c + 1]),
                             reads=[bt, bsp, bE], writes=[bE])
                        S.op("dve", lambda e: e.scalar_tensor_tensor(out=EMr[:], in0=TB(5), scalar=FI[:, c:c + 1], in1=EMr[:],
                                                                     op0=ALU.mult, op1=ALU.add), reads=[bt, bsp, bE], writes=[bE])
                        S.op("act", lambda e: e.activation(out=TB(1), in_=TB(5), func=AF.Identity, scale=FR[:, c:c + 1]),
                             reads=[bt, bsp], writes=[bt])
                        S.op("dve", lambda e: e.scalar_tensor_tensor(out=EMi[:], in0=TB(4), scalar=FI[:, c:c + 1], in1=TB(1),
                                                                     op0=ALU.mult, op1=ALU.subtract), reads=[bt, bsp, bE], writes=[bE])
                        for hf in range(2):
                            hs = slice(hf * 512, (hf + 1) * 512)
                            mm(psh(0, hf), bm_t[sl][:, (d_ * 2) * 128:(d_ * 2 + 1) * 128], OS[:, kt, hs],
                               True, True, [bbm[sl], bOS[kt]], [bPS[0][hf]])
                            mm(psh(1, hf), bm_t[sl][:, (d_ * 2 + 1) * 128:(d_ * 2 + 2) * 128], OS[:, kt, hs],
                               True, True, [bbm[sl], bOS[kt]], [bPS[1][hf]])
                        S.op("pool", lambda e: e.tensor_scalar(out=pc_[:, 6:7], in0=EPi[:, 255:256], scalar1=-1.0,
                                                               scalar2=None, op0=ALU.mult), reads=[bE, bpc], writes=[bpc])
                        S.op("pool", lambda e: e.tensor_copy(out=pc_[:, 7:8], in_=EPi[:, 255:256]), reads=[bE, bpc], writes=[bpc])
                        h0c = H0 + (d_ * 24 + gp) * 2
                        first_chunk = 0 if not r_ else 3
                        S.op("pool", lambda e: e.tensor_copy(out=pc_[:, 8 + 2 * first_chunk:10 + 2 * first_chunk],
                                                             in_=ssmp[:, h0c:h0c + 2]), reads=[bC, bpc], writes=[bpc])
                        fw = (lambda a: a) if not r_ else rev
                        for hf in ([0, 1] if not r_ else [1, 0]):
                            hs = slice(hf * 512, (hf + 1) * 512)
                            bur, bui = v2(psh(0, hf)), v2(psh(1, hf))
                            er, ei = chunked(EMr[:, :], 2, r_), chunked(EMi[:, :], 2, r_)
                            S.op("dve", lambda e: e.tensor_tensor(out=v2(W1[:, :]), in0=bur, in1=er, op=ALU.mult), reads=[bPS[0][hf], bE, bW1], writes=[bW1])
                            S.op("dve", lambda e: e.tensor_tensor(out=v2(DT[:, :]), in0=bui, in1=ei, op=ALU.mult), reads=[bPS[1][hf], bE, bDT], writes=[bDT])
                            S.op("dve", lambda e: e.tensor_tensor(out=W1[:], in0=W1[:], in1=DT[:], op=ALU.subtract), reads=[bW1, bDT], writes=[bW1])
                            S.op("dve", lambda e: e.tensor_tensor(out=v2(ZR[:, :]), in0=bui, in1=er, op=ALU.mult), reads=[bPS[1][hf], bE, bZR], writes=[bZR])
                            S.op("dve", lambda e: e.tensor_tensor(out=v2(DT[:, :]), in0=bur, in1=ei, op=ALU.mult), reads=[bPS[0][hf], bE, bDT], writes=[bDT])
                            S.op("dve", lambda e: e.tensor_tensor(out=ZR[:], in0=ZR[:], in1=DT[:], op=ALU.add), reads=[bZR, bDT], writes=[bZR])
                            S.op("dve", lambda e: e.tensor_tensor_scan(out=fw(W2[:, :]), data0=rmask[:, :], data1=fw(W1[:, :]),
                                                                       initial=0.0, op0=ALU.mult, op1=ALU.add),
                                 reads=[bW1, bC, bW2], writes=[bW2])
                            S.op("dve", lambda e: e.tensor_tensor_scan(out=fw(ZI[:, :]), data0=rmask[:, :], data1=fw(ZR[:, :]),
                                                                       initial=0.0, op0=ALU.mult, op1=ALU.add),
                                 reads=[bZR, bC, bZI], writes=[bZI])
                            chunks = [2 * hf, 2 * hf + 1] if not r_ else [2 * hf + 1, 2 * hf]
                            for cch in chunks:
                                cl = cch - 2 * hf
                                endcol = cl * 256 + (255 if not r_ else 0)
                                pcol = 8 + 2 * cch
                                S.op("pool", lambda e: e.tensor_tensor(out=pc_[:, 2:3], in0=W2[:, endcol:endcol + 1],
                                                                       in1=pc_[:, pcol:pcol + 1], op=ALU.add), reads=[bW2, bpc], writes=[bpc])
                                S.op("pool", lambda e: e.tensor_tensor(out=pc_[:, 3:4], in0=ZI[:, endcol:endcol + 1],
                                                                       in1=pc_[:, pcol + 1:pcol + 2], op=ALU.add), reads=[bZI, bpc], writes=[bpc])
                                S.op("pool", lambda e: e.tensor_tensor(out=pc_[:, 4:6], in0=rev(pc_[:, 2:4]), in1=pc_[:, 6:8],
                                                                       op=ALU.mult), reads=[bpc], writes=[bpc])
                                fcol = ((cch * 2 + d_) * 24 + gp) * 2
                                S.op("pool", lambda e: e.tensor_scalar(out=pc_[:, 2:4], in0=pc_[:, 2:4], scalar1=EPr[:, 255:256],
                                                                       scalar2=None, op0=ALU.mult), reads=[bpc, bE], writes=[bpc])
                                S.op("pool", lambda e: e.tensor_tensor(out=fin[:, fcol:fcol + 2], in0=pc_[:, 2:4], in1=pc_[:, 4:6],
                                                                       op=ALU.add), reads=[bpc, bFin], writes=[bFin])
                                nxt = cch + (1 if not r_ else -1)
                                if 0 <= nxt < 4:
                                    ncol = 8 + 2 * nxt
                                    S.op("pool", lambda e: e.tensor_scalar(out=pc_[:, ncol:ncol + 2], in0=fin[:, fcol:fcol + 2],
                                                                           scalar1=SM("flags", 0, 1), scalar2=None, op0=ALU.mult),
                                         reads=[bFin, bC, bpc], writes=[bpc])
                            for cl in range(2):
                                pcol = 8 + 2 * (2 * hf + cl)
                                cs = slice(cl * 256, (cl + 1) * 256)
                                S.op("act", lambda e: e.activation(out=W2[:, cs], in_=W2[:, cs], func=AF.Identity,
                                                                   bias=pc_[:, pcol:pcol + 1]), reads=[bW2, bpc], writes=[bW2])
                                S.op("act", lambda e: e.activation(out=ZI[:, cs], in_=ZI[:, cs], func=AF.Identity,
                                                                   bias=pc_[:, pcol + 1:pcol + 2]), reads=[bZI, bpc], writes=[bZI])
                            er, ei = chunked(EPr[:, :], 2, r_), chunked(EPi[:, :], 2, r_)
                            S.op("pool", lambda e: e.tensor_tensor(out=v2(P1[:, :]), in0=v2(W2[:, :]), in1=er, op=ALU.mult), reads=[bW2, bE, bP1], writes=[bP1])
                            S.op("pool", lambda e: e.tensor_tensor(out=v2(P2[:, :]), in0=v2(ZI[:, :]), in1=ei, op=ALU.mult), reads=[bZI, bE, bP2], writes=[bP2])
                            S.op("pool", lambda e: e.tensor_tensor(out=XR[:], in0=P1[:], in1=P2[:], op=ALU.subtract), reads=[bP1, bP2, bX], writes=[bX])
                            S.op("pool", lambda e: e.tensor_tensor(out=v2(P1[:, :]), in0=v2(ZI[:, :]), in1=er, op=ALU.mult), reads=[bZI, bE, bP1], writes=[bP1])
                            S.op("pool", lambda e: e.tensor_tensor(out=v2(P2[:, :]), in0=v2(W2[:, :]), in1=ei, op=ALU.mult), reads=[bW2, bE, bP2], writes=[bP2])
                            S.op("pool", lambda e: e.tensor_tensor(out=P1[:], in0=P1[:], in1=P2[:], op=ALU.add), reads=[bP1, bP2], writes=[bP1])
                            S.op("pool", lambda e: e.tensor_scalar(out=XI[:], in0=P1[:], scalar1=-1.0, scalar2=None, op0=ALU.mult),
                                 reads=[bP1, bX], writes=[bX])
                            first = (g4 == 0 and d_ == 0)
                            last = (g4 == 3 and d_ == 1)
                            mm(psh(2, hf), cm_t[sl][:, (d_ * 2) * 128:(d_ * 2 + 1) * 128], XR[:], first, False,
                               [bbm[sl], bX], [bPS[2][hf]], inc=False)
                            mm(psh(2, hf), cm_t[sl][:, (d_ * 2 + 1) * 128:(d_ * 2 + 2) * 128], XI[:], False, last,
                               [bbm[sl], bX], [bPS[2][hf]], inc=True)
                for hf in range(2):
                    hs = slice(hf * 512, (hf + 1) * 512)
                    S.op("dve", lambda e: e.scalar_tensor_tensor(out=ytmp[:], in0=OS[:, kt, hs], scalar=SM("ssmd", l * 6 + kt, 1),
                                                                 in1=psh(2, hf), op0=ALU.mult, op1=ALU.add),
                         reads=[bOS[kt], bC, bPS[2][hf], byt], writes=[byt])
                    S.op("dve", lambda e: e.tensor_tensor(out=W1[:], in0=ytmp[:], in1=ytmp[:], op=ALU.mult), reads=[byt, bW1], writes=[bW1])
                    S.op("dve", lambda e: e.tensor_scalar(out=W1[:], in0=W1[:], scalar1=0.044715, scalar2=1.0, op0=ALU.mult,
                                                          op1=ALU.add), reads=[bW1], writes=[bW1])
                    S.op("dve", lambda e: e.tensor_tensor(out=W1[:], in0=W1[:], in1=ytmp[:], op=ALU.mult), reads=[bW1, byt], writes=[bW1])
                    S.op("act", lambda e: e.activation(out=W1[:], in_=W1[:], func=AF.Sigmoid, scale=2.0 * 0.7978845608),
                         reads=[bW1], writes=[bW1])
                    S.op("dve", lambda e: e.tensor_tensor(out=YS[:, kt, hs], in0=W1[:], in1=ytmp[:], op=ALU.mult),
                         reads=[bW1, byt], writes=[bYS[kt]])
            for _ in modgen:
                pass
            S.dma("sp", o_fin[:, l * 384:(l + 1) * 384], fin[:], reads=[bFin], writes=[bOut])
            for m0 in range(0, 6, 2):
                wp, wb = load_w([(d_wglu[l][:, m0 * 128:(m0 + 2) * 128], 0)], 6, 256)
                for j in range(2):
                    mt = m0 + j
                    for hf in range(2):
                        hs = slice(hf * 512, (hf + 1) * 512)
                        for k6 in range(6):
                            mm(psh(0, hf), wp[:, k6, j * 128:(j + 1) * 128], YS[:, k6, hs], k6 == 0, k6 == 5,
                               [wb, bYS[k6]], [bPS[0][hf]])
                        S.op("act", lambda e: e.activation(out=ytmp[:], in_=psh(0, hf), func=AF.Sigmoid,
                                                           bias=SM("bglu", l * 6 + mt, 1)), reads=[bPS[0][hf], bC, byt], writes=[byt])
                        S.op("dve", lambda e: e.tensor_tensor(out=OS[:, mt, hs], in0=ytmp[:], in1=YS[:, mt, hs], op=ALU.mult),
                             reads=[byt, bYS[mt]], writes=[bOS[mt]])

        def rope_tile(tile, bt_, mt, scale, stk_tmp):
            (t1, t2, bt1, ropet) = stk_tmp
            for hf in range(2):
                hs = slice(hf * 512, (hf + 1) * 512)
                mm(psh(0, hf), pswap[:], tile[:, mt, hs], True, True, [bC, bt_], [bPS[0][hf]])
                S.op("dve", lambda e: e.scalar_tensor_tensor(out=t1[:], in0=tile[:, mt, hs], scalar=scale,
                                                             in1=ropet[:, hf * 512:(hf + 1) * 512], op0=ALU.mult, op1=ALU.mult),
                     reads=[bt_, bC, bt1], writes=[bt1])
                S.op("dve", lambda e: e.scalar_tensor_tensor(out=t2[:], in0=psh(0, hf), scalar=scale,
                                                             in1=ropet[:, T + hf * 512:T + (hf + 1) * 512], op0=ALU.mult, op1=ALU.mult),
                     reads=[bPS[0][hf], bC, bt1], writes=[bt1])
                S.op("dve", lambda e: e.tensor_tensor(out=tile[:, mt, hs], in0=t1[:], in1=t2[:], op=ALU.add),
                     reads=[bt1], writes=[bt_])

        def win_branch(l, OW, bOW, wk):
            KW = sb("KW", [128, 4, T], BF16, wk)
            bKW = [Buf() for _ in range(4)]
            VW = sb("VW", [128, 8 * 256 + 64], BF16, wk)
            VWv = VW[:, 0:2048].rearrange("p (b c) -> p b c", b=8)
            bVW = Buf()
            KC = sb("KWc", [128, 4, 512], BF16, wk)
            VC = sb("VWc", [128, 4 * 256 + 64], BF16, wk)
            bKC, bVC = Buf(), Buf()
            pt = [sb("wpt%d" % i, [128, 1152], BF16, wk) for i in range(2)]
            bpt = [Buf(), Buf()]
            rinv = [sb("wri%d" % i, [128, 128], F32, wk) for i in range(2)]
            brinv = [Buf(), Buf()]
            t1 = sb("wt1", [128, 512], F32, wk)
            t2 = sb("wt2", [128, 512], F32, wk)
            bt1 = Buf()
            ropet = sb("ropet", [128, 2 * T], BF16, wk)
            wmask = sb("wmask_s", [128, 24 * 128], BF16, wk)
            S.dma("pool", ropet[:], d_rope[:, :], writes=[bC])
            S.dma("pool", wmask[:], d_wmask[:, :], writes=[bC])
            stg = ([sb("wstg%d" % i, [128, 256], F32, wk) for i in range(2)], [Buf(), Buf()])
            S.op("pool", lambda e: e.memset(VW[:, 2048:2112], 1.0), writes=[bVW])
            S.op("pool", lambda e: e.memset(VC[:, 1024:1088], 1.0), writes=[bVC])
            S.dma("pool", KC[:], d_kwc[l].rearrange("(k p) c -> p k c", p=128), writes=[bKC])
            S.dma("pool", VC[:, 0:1024].rearrange("p (b c) -> p b c", b=4), d_vwc[l].rearrange("(b p) c -> p b c", p=128),
                  writes=[bVC])
            S.op("act", lambda e: e.activation(out=SM("sinke", 0, 12), in_=SM("sink", l * 12, 12), func=AF.Exp),
                 reads=[bC, bSmall], writes=[bSmall])

            def ev_q(mt, hf, ps, pb):
                S.op("act", lambda e: e.activation(out=OW[:, mt, hf * 512:(hf + 1) * 512], in_=ps, func=AF.Copy),
                     reads=[pb], writes=[bOW[mt]])

            def ev_k(mt, hf, ps, pb):
                S.op("act", lambda e: e.activation(out=KW[:, mt, hf * 512:(hf + 1) * 512], in_=ps, func=AF.Copy),
                     reads=[pb], writes=[bKW[mt]])
            proj_fm(l, [[(OFF_QW + m * 128, 128)] for m in range(6)], ev_q)
            proj_fm(l, [[(OFF_KW + j * 64, 64), (OFF_KW + j * 64, 64)] for j in range(4)], ev_k)
            if STOP < 4.07:
                return
            tokmajor(l, OFF_KW, 256, 0, None, stg)
            tokmajor(l, OFF_VW, 256, 256, (VWv, bVW, 0), stg)
            if STOP < 4.15:
                return
            for mt in range(6):
                rope_tile(OW, bOW[mt], mt, 0.125, (t1, t2, bt1, ropet))
            for mt in range(4):
                rope_tile(KW, bKW[mt], mt, 1.0, (t1, t2, bt1, ropet))
            if STOP < 4.25:
                return
            for h in range(12):
                kv = h // 3
                qt, qh = h // 2, h % 2

                def units(n):
                    u = []
                    for rel in (-1, 0, 1):
                        b = n + rel
                        if 0 <= b < 8:
                            u.append((b, [wmask[:, (n * 3 + rel + 1) * 128:(n * 3 + rel + 2) * 128]]))
                    return u
                attn_head(OW, bOW[qt], qt, qh, KW, bKW[kv], kv, KC, bKC, kv,
                          lambda b: vaug_ap(VW, b * 256 + kv * 64, 2048), lambda cb: vaug_ap(VC, cb * 256 + kv * 64, 1024),
                          bVW, bVC, units, SM("sinke", h, 1), (pt, bpt, rinv, brinv))

        def na_branch(l, hg, ON, bON, nk):
            KN = sb("KN", [128, 3, T], BF16, nk)
            bKN = [Buf() for _ in range(3)]
            VN = sb("VN", [128, 8 * 384 + 64], BF16, nk)
            VNv = VN[:, 0:3072].rearrange("p (b c) -> p b c", b=8)
            bVN = Buf()
            KC = sb("KNc", [128, 3, 512], BF16, nk)
            VC = sb("VNc", [128, 4 * 384 + 64], BF16, nk)
            bKC, bVC = Buf(), Buf()
            RR = [sb("nar%d" % i, [128, 7 * 128], BF16, nk) for i in range(2)]
            bRR = [Buf(), Buf()]
            pt = [sb("npt%d" % i, [128, 1152], BF16, nk) for i in range(2)]
            bpt = [Buf(), Buf()]
            rinv = [sb("nri%d" % i, [128, 128], F32, nk) for i in range(2)]
            brinv = [Buf(), Buf()]
            nmask = sb("nmask_s", [128, N_NAM * 128], BF16, nk)
            S.dma("pool", nmask[:], d_nmask[:, :], writes=[bC])
            stg = ([sb("nstg%d" % i, [128, 256], F32, nk) for i in range(2)], [Buf(), Buf()])
            S.op("pool", lambda e: e.memset(VN[:, 3072:3136], 1.0), writes=[bVN])
            S.op("pool", lambda e: e.memset(VC[:, 1536:1600], 1.0), writes=[bVC])
            S.dma("pool", KC[:], d_knc[l][hg * 384:(hg + 1) * 384, :].rearrange("(k p) c -> p k c", p=128), writes=[bKC])
            S.dma("pool", VC[:, 0:1536].rearrange("p (b c) -> p b c", b=4),
                  d_vnc[l][:, hg * 384:(hg + 1) * 384].rearrange("(b p) c -> p b c", p=128), writes=[bVC])

            def ev_q(mt, hf, ps, pb):
                S.op("act", lambda e: e.activation(out=ON[:, hg * 3 + mt, hf * 512:(hf + 1) * 512], in_=ps, func=AF.Copy,
                                                   scale=0.125), reads=[pb], writes=[bON[hg * 3 + mt]])

            def ev_k(mt, hf, ps, pb):
                S.op("act", lambda e: e.activation(out=KN[:, mt, hf * 512:(hf + 1) * 512], in_=ps, func=AF.Copy),
                     reads=[pb], writes=[bKN[mt]])
            proj_fm(l, [[(OFF_QN + (hg * 3 + m) * 128, 128)] for m in range(3)], ev_q)
            proj_fm(l, [[(OFF_KN + (hg * 3 + m) * 128, 128)] for m in range(3)], ev_k)
            for c0 in (0, 256):
                w = 256 if c0 == 0 else 128
                tokmajor(l, OFF_KN + hg * 384 + c0, w, 512 + hg * 384 + c0, None, stg)
                tokmajor(l, OFF_VN + hg * 384 + c0, w, 1280 + hg * 384 + c0, (VNv, bVN, c0), stg)
            for hl in range(6):
                h = hg * 6 + hl
                qt, qh = h // 2, h % 2
                lt = hl // 2
                sl = hl % 2
                S.dma("pool", RR[sl][:], d_nar[l][h], writes=[bRR[sl]])

                def units(n, sl=sl):
                    u = []
                    for b in NA_KB[n]:
                        idx = NA_IDX[(n, b)]
                        dl = b - n + 3
                        u.append((b, [nmask[:, idx * 128:(idx + 1) * 128], RR[sl][:, dl * 128:(dl + 1) * 128]]))
                    return u
                attn_head_na(ON, bON[qt], qt, qh, KN, bKN[lt], lt, KC, bKC, lt,
                             lambda b, hl=hl: vaug_ap(VN, b * 384 + hl * 64, 3072),
                             lambda cb, hl=hl: vaug_ap(VC, cb * 384 + hl * 64, 1536),
                             bVN, bVC, units, None, (pt, bpt, rinv, brinv), bRR[sl])

        def attn_head_na(*args):
            extra = args[-1]
            old = attn_extra[0]
            attn_extra[0] = extra
            attn_head(*args[:-1])
            attn_extra[0] = old

        attn_extra = [None]
        _mm_orig = mm

        def mm(out, lhsT, rhs, start, stop, reads, writes, inc=None):
            if attn_extra[0] is not None:
                reads = list(reads) + [attn_extra[0]]
            _mm_orig(out, lhsT, rhs, start, stop, reads, writes, inc)

        def merge(l, OS, bOS, OW, bOW, ON, bON, gk):
            MT = [sb("mT%d" % i, [128, 4, T], BF16, gk) for i in range(2)]
            bMT = [Buf(), Buf()]
            acc = sb("macc", [128, T], F32, gk)
            bacc = Buf()
            sg = sb("msg", [128, T], F32, gk)
            bsg = Buf()
            OB = [(OS, bOS), (OW, bOW), (ON, bON)]
            for fg in range(4):
                mi = fg % 2
                for fl in range(4):
                    ft = fg * 4 + fl
                    for x in range(3):
                        wp, wb = load_w([(d_win[l][:, OFF_G + x * D + ft * 128:OFF_G + x * D + (ft + 1) * 128], 0)], 16, 128)
                        wq, wqb = load_w([(d_wbr[l][x][:, ft * 128:(ft + 1) * 128], 0)], 6, 128)
                        O_, bO_ = OB[x]
                        for hf in range(2):
                            hs = slice(hf * 512, (hf + 1) * 512)
                            for kt in range(16):
                                mm(psh(0, hf), wp[:, kt, :], HT[:, kt, hs], kt == 0, kt == 15, [wb, bHT[kt]], [bPS[0][hf]])
                            for k6 in range(6):
                                mm(psh(1, hf), wq[:, k6, :], O_[:, k6, hs], k6 == 0, k6 == 5, [wqb, bO_[k6]], [bPS[1][hf]])
                        S.op("act", lambda e: e.activation(out=sg[:], in_=PS[0][:, :], func=AF.Sigmoid), reads=bPS[0] + [bsg],
                             writes=[bsg])
                        if x == 0:
                            S.op("dve", lambda e: e.tensor_tensor(out=acc[:], in0=PS[1][:, :], in1=sg[:], op=ALU.mult),
                                 reads=bPS[1] + [bsg, bacc], writes=[bacc])
                        else:
                            S.op("dve", lambda e: e.tensor_tensor(out=sg[:], in0=PS[1][:, :], in1=sg[:], op=ALU.mult),
                                 reads=bPS[1] + [bsg], writes=[bsg])
                            if x == 1:
                                S.op("dve", lambda e: e.tensor_tensor(out=acc[:], in0=acc[:], in1=sg[:], op=ALU.add),
                                     reads=[bsg, bacc], writes=[bacc])
                            else:
                                S.op("dve", lambda e: e.tensor_tensor(out=MT[mi][:, fl, :], in0=acc[:], in1=sg[:], op=ALU.add),
                                     reads=[bsg, bacc], writes=[bMT[mi]])
                for mc in range(2):
                    wp, wb = load_w([(d_wout[l][fg * 512:(fg + 1) * 512, mc * 1024:(mc + 1) * 1024], 0)], 4, 1024)
                    for mj in range(8):
                        mo = mc * 8 + mj
                        for hf in range(2):
                            hs = slice(hf * 512, (hf + 1) * 512)
                            for k4 in range(4):
                                mm(psh(2, hf), wp[:, k4, mj * 128:(mj + 1) * 128], MT[mi][:, k4, hs], k4 == 0, k4 == 3,
                                   [wb, bMT[mi]], [bPS[2][hf]])
                        S.op("dve", lambda e: e.scalar_tensor_tensor(out=XT[:, mo, :], in0=PS[2][:, :], scalar=SM("modv%d" % (l % 2), 32 + mo, 1),
                                                                     in1=XT[:, mo, :], op0=ALU.mult, op1=ALU.add),
                             reads=bPS[2] + [bSmall, bXT[mo]], writes=[bXT[mo]])

        def ffn(l, fk):
            HID = [sb("hid%d" % i, [128, 11, T], BF16, fk) for i in range(2)]
            bHID = [Buf(), Buf()]
            acc_a = sb("facc_a", [128, T], F32, fk)
            acc_b = sb("facc_b", [128, T], F32, fk)
            ba, bb = Buf(), Buf()
            cw = lambda k, ch: big2[:, CW0 + (l * 3 + k) * 88 + ch:CW0 + (l * 3 + k) * 88 + ch + 1]
            cb = lambda ch: big2[:, CB0 + l * 88 + ch:CB0 + l * 88 + ch + 1]
            S.op("dve", lambda e: e.tensor_scalar(out=SM("nw0", 0, 88), in0=big2[:, CW0 + (l * 3) * 88:CW0 + (l * 3 + 1) * 88],
                                                  scalar1=SM("flags", 1, 1), scalar2=None, op0=ALU.mult), reads=[bC, bSmall], writes=[bSmall])
            S.op("dve", lambda e: e.tensor_scalar(out=SM("nw2", 0, 88), in0=big2[:, CW0 + (l * 3 + 2) * 88:CW0 + (l * 3 + 3) * 88],
                                                  scalar1=SM("flags", 1, 1), scalar2=None, op0=ALU.mult), reads=[bC, bSmall], writes=[bSmall])

            def bcols(ap2d, start):
                c = ap2d[:, start:start + 1]
                (ps_, pc), (fs, fc) = c.ap
                return AP(c.tensor, c.offset, [[ps_, pc], [256 * fs, 3]])

            for g in range(4):
                hi = g % 2
                for il in range(11):
                    i = g * 11 + il
                    wp, wb = load_w([(d_wup[l][:, i * 128:(i + 1) * 128], 0), (d_wup[l][:, DFF + i * 128:DFF + (i + 1) * 128], 128)], 16, 256)
                    for ab in range(2):
                        ch = i + 44 * ab
                        pi = ab
                        for hf in range(2):
                            hs = slice(hf * 512, (hf + 1) * 512)
                            for kt in range(16):
                                mm(psh(pi, hf), wp[:, kt, ab * 128:(ab + 1) * 128], HT[:, kt, hs], kt == 0, kt == 15,
                                   [wb, bHT[kt]], [bPS[pi][hf]])
                        acc, bacc = (acc_a, ba) if ab == 0 else (acc_b, bb)
                        S.op("act", lambda e: e.activation(out=acc[:], in_=PS[pi][:, :], func=AF.Identity, scale=cw(1, ch), bias=cb(ch)),
                             reads=bPS[pi] + [bC, bacc], writes=[bacc])
                        S.op("dve", lambda e: e.scalar_tensor_tensor(out=acc[:, 1:T], in0=PS[pi][:, 0:T - 1], scalar=cw(0, ch),
                                                                     in1=acc[:, 1:T], op0=ALU.mult, op1=ALU.add),
                             reads=bPS[pi] + [bC, bacc], writes=[bacc])
                        S.op("dve", lambda e: e.scalar_tensor_tensor(out=acc[:, 0:T - 1], in0=PS[pi][:, 1:T], scalar=cw(2, ch),
                                                                     in1=acc[:, 0:T - 1], op0=ALU.mult, op1=ALU.add),
                             reads=bPS[pi] + [bC, bacc], writes=[bacc])
                        S.op("dve", lambda e: e.scalar_tensor_tensor(out=bcols(acc[:, :], 256), in0=bcols(PS[pi][:, :], 255),
                                                                     scalar=SM("nw0", ch, 1), in1=bcols(acc[:, :], 256),
                                                                     op0=ALU.mult, op1=ALU.add),
                             reads=bPS[pi] + [bSmall, bacc], writes=[bacc])
                        S.op("dve", lambda e: e.scalar_tensor_tensor(out=bcols(acc[:, :], 255), in0=bcols(PS[pi][:, :], 256),
                                                                     scalar=SM("nw2", ch, 1), in1=bcols(acc[:, :], 255),
                                                                     op0=ALU.mult, op1=ALU.add),
                             reads=bPS[pi] + [bSmall, bacc], writes=[bacc])
                    S.op("act", lambda e: e.activation(out=acc_a[:], in_=acc_a[:], func=AF.Silu), reads=[ba], writes=[ba])
                    S.op("dve", lambda e: e.tensor_tensor(out=HID[hi][:, il, :], in0=acc_a[:], in1=acc_b[:], op=ALU.mult),
                         reads=[ba, bb], writes=[bHID[hi]])
                for mc in range(8):
                    wp, wb = load_w([(d_wdn[l][g * 1408:(g + 1) * 1408, mc * 256:(mc + 1) * 256], 0)], 11, 256)
                    for mj in range(2):
                        mo = mc * 2 + mj
                        for hf in range(2):
                            hs = slice(hf * 512, (hf + 1) * 512)
                            for k in range(11):
                                mm(psh(2, hf), wp[:, k, mj * 128:(mj + 1) * 128], HID[hi][:, k, hs], k == 0, k == 10,
                                   [wb, bHID[hi]], [bPS[2][hf]])
                        S.op("dve", lambda e: e.scalar_tensor_tensor(out=XT[:, mo, :], in0=PS[2][:, :], scalar=SM("modv%d" % (l % 2), 80 + mo, 1),
                                                                     in1=XT[:, mo, :], op0=ALU.mult, op1=ALU.add),
                             reads=bPS[2] + [bSmall, bXT[mo]], writes=[bXT[mo]])

        for l in range(depth):
            layer(l)
        with contextlib.ExitStack() as stk:
            phase_norm("fg", 0, None, 0, stk, final=True)
            S.barrier()
        print("instructions:", S.n_ins, {k: v for k, v in S.cnt.items()})
    return nc


def _consts():
    ident = np.eye(128, dtype=np.float32)
    psw = np.zeros((128, 128), np.float32)
    for p in range(128):
        i = p % 64
        j = i + 16 if (i % 32) < 16 else i - 16
        psw[(p // 64) * 64 + j, p] = 1.0
    ones = np.ones((128, 128), np.float32)
    tt = np.tile(np.arange(1, 257, dtype=np.float32)[None], (128, 1))
    rm = np.ones((128, T), np.float32)
    rm[:, 0::256] = 0.0
    return np.concatenate([ident, psw, ones, tt, rm], axis=1)


def _rope_tables(sample):
    cos = np.ones((128, T), np.float32)
    sin = np.zeros((128, T), np.float32)
    if sample:
        t = np.arange(T)
        row = (t // 64).astype(np.float32)
        colp = (t % 64).astype(np.float32)
        inv = (10000.0 ** (-np.arange(16, dtype=np.float32) / 16)).astype(np.float32)
        for p in range(128):
            i = p % 64
            j = i % 16
            pos = row if i < 32 else colp
            ang = (pos * inv[j]).astype(np.float32)
            cos[p] = np.cos(ang)
            sgn = -1.0 if (i % 32) < 16 else 1.0
            sin[p] = sgn * np.sin(ang)
    return np.concatenate([cos, sin], axis=1).astype(np.float32)


def _win_masks(sample):
    m = np.full((24, 128, 128), NEG, np.float32)
    k = np.arange(128)[:, None]
    q = np.arange(128)[None, :]
    for n in range(8):
        for rel in (-1, 0, 1):
            b = n + rel
            if not (0 <= b < 8):
                continue
            if sample:
                if rel == 0:
                    ok = np.ones((128, 128), bool)
                elif rel == -1:
                    ok = q <= k
                else:
                    ok = k <= q
            else:
                ok = np.full((128, 128), (b // 2) == (n // 2))
            m[n * 3 + rel + 1] = np.where(ok, 0.0, NEG)
    return np.ascontiguousarray(m.transpose(1, 0, 2).reshape(128, 24 * 128))


def _na_masks(sample):
    m = np.full((N_NAM, 128, 128), NEG, np.float32)
    for (j, b), idx in NA_IDX.items():
        for ka in range(2):
            for qa in range(2):
                rq, rk = 2 * j + qa, 2 * b + ka
                if sample:
                    rs = min(max(rq - 4, 0), 8)
                    ok = rs <= rk < rs + 8
                else:
                    ok = (rq // 4) == (rk // 4)
                if ok:
                    m[idx, ka * 64:(ka + 1) * 64, qa * 64:(qa + 1) * 64] = 0.0
    return np.ascontiguousarray(m.transpose(1, 0, 2).reshape(128, N_NAM * 128))


def _na_r(rpb_l):
    out = np.zeros((12, 7, 128, 128), np.float32)
    ck = np.arange(64)[:, None]
    cq = np.arange(64)[None, :]
    qs = np.clip(cq - 8, 0, 48)
    valid = (ck >= qs) & (ck < qs + 16)
    dc = np.clip(ck - cq + 15, 0, 30)
    for dl in range(7):
        for ka in range(2):
            for qa in range(2):
                dr = 2 * (dl - 3) + ka - qa
                if -7 <= dr <= 7:
                    blk = np.where(valid[None], rpb_l[:, dr + 7][:, dc], NEG)
                else:
                    blk = np.full((12, 64, 64), NEG, np.float32)
                out[:, dl, ka * 64:(ka + 1) * 64, qa * 64:(qa + 1) * 64] = blk
    return np.ascontiguousarray(out.transpose(0, 2, 1, 3).reshape(12, 128, 7 * 128))


def _pcol(a):
    a = np.asarray(a, np.float32)
    lead = a.shape[:-1]
    n = a.shape[-1] // 128
    b = a.reshape(lead + (n, 128))
    b = np.moveaxis(b, -1, 0)
    return np.ascontiguousarray(b.reshape(128, -1))


def _ssm_state_layout(a):
    a = np.asarray(a, np.float32)
    lead = a.shape[:-2]
    b = a.reshape(lead + (24, 2, 64))
    b = np.moveaxis(b, -3, -1)
    b = b.reshape(lead + (128, 24))
    b = np.moveaxis(b, -2, 0)
    return np.ascontiguousarray(b)


def prepare(inputs, depth=DEPTH):
    f = lambda k: np.asarray(inputs[k], np.float32)
    shared = {}
    shared["w_mod"] = f("w_mod")[:depth]
    shared["b_mod"] = _pcol(f("b_mod")[:depth])
    shared["n1g"] = _pcol(f("norm1_g")[:depth])
    shared["n2g"] = _pcol(f("norm2_g")[:depth])
    shared["fg"] = _pcol(f("final_g"))
    shared["w_in"] = f("w_in")[:depth]
    shared["w_branch"] = f("w_branch")[:depth]
    shared["w_out"] = f("w_out")[:depth]
    shared["w_up"] = f("w_up")[:depth]
    shared["w_down"] = f("w_down")[:depth]
    shared["w_glu"] = f("w_glu")[:depth]
    shared["b_glu"] = _pcol(f("b_glu")[:depth])
    shared["conv_w"] = _pcol(f("conv_w")[:depth])
    shared["conv_b"] = _pcol(f("conv_b")[:depth])
    shared["ssm_d"] = _pcol(f("ssm_d")[:depth].reshape(depth, 768))
    shared["sink"] = np.ascontiguousarray(np.tile(f("win_sink")[:depth].reshape(1, depth * 12), (128, 1)))
    shared["lam_re"] = _ssm_state_layout(f("ssm_lam_re")[:depth]).reshape(128, depth * 48)
    shared["lam_im"] = _ssm_state_layout(f("ssm_lam_im")[:depth]).reshape(128, depth * 48)
    ldt = np.tile(f("ssm_log_dt")[:depth][..., None], (1, 1, 1, 64))
    shared["log_dt"] = _ssm_state_layout(ldt).reshape(128, depth * 48)
    bre, bim = f("ssm_b_re")[:depth], f("ssm_b_im")[:depth]
    cre, cim = f("ssm_c_re")[:depth], f("ssm_c_im")[:depth]
    bmat = np.zeros((depth, 24, 128, 4, 128), np.float32)
    cmat = np.zeros((depth, 24, 128, 4, 128), np.float32)
    for gp in range(24):
        for gl in range(2):
            g = gp * 2 + gl
            r0 = (gp % 4) * 32 + gl * 16
            for d_ in range(2):
                for ri, (bb, cc) in enumerate(((bre, cre), (bim, cim))):
                    bmat[:, gp, r0:r0 + 16, d_ * 2 + ri, gl * 64:(gl + 1) * 64] = bb[:, d_, g].transpose(0, 2, 1)
                    cmat[:, gp, gl * 64:(gl + 1) * 64, d_ * 2 + ri, r0:r0 + 16] = cc[:, d_, g].transpose(0, 2, 1)
    shared["bmat"] = bmat.reshape(depth, 24, 128, 512)
    shared["cmat"] = cmat.reshape(depth, 24, 128, 512)
    shared["consts"] = _consts()
    rpb = f("na_rpb")[:depth]
    nar_s = np.stack([_na_r(rpb[l]) for l in range(depth)])
    nar_p = np.zeros_like(nar_s)
    per_type = {}
    for sample in (False, True):
        per_type[sample] = {"rope": _rope_tables(sample), "wmask": _win_masks(sample), "nmask": _na_masks(sample),
                            "nar": nar_s if sample else nar_p}
    xp, xs = f("x_prompt"), f("x_sample")
    in_maps = []
    for c in range(8):
        sample = c >= 4
        m = dict(shared)
        m.update(per_type[sample])
        flags = np.zeros((128, 4), np.float32)
        if sample:
            b = c - 4
            m["xT"] = np.ascontiguousarray(xs[b].T)
            m["cond"] = _pcol(f("c")[b])
            flags[:, 0] = 1.0
            hre = _ssm_state_layout(f("state_ssm_re")[b, :depth])
            him = _ssm_state_layout(f("state_ssm_im")[b, :depth])
            m["h0"] = np.ascontiguousarray(np.stack([hre, him], axis=-1).reshape(128, -1))
            kw = f("cache_win_k")[b, :depth].reshape(depth, 512, 4, 64)
            kw = np.repeat(kw.transpose(0, 2, 3, 1)[:, :, None], 2, axis=2)
            m["kwc"] = np.ascontiguousarray(kw.reshape(depth, 512, 512))
            m["knc"] = np.ascontiguousarray(f("cache_na_k")[b, :depth].reshape(depth, 512, 768).transpose(0, 2, 1))
            m["vwc"] = np.ascontiguousarray(f("cache_win_v")[b, :depth].reshape(depth, 512, 256))
            m["vnc"] = np.ascontiguousarray(f("cache_na_v")[b, :depth].reshape(depth, 512, 768))
        else:
            m["xT"] = np.ascontiguousarray(xp[4 * c:4 * c + 4].reshape(T, D).T)
            m["cond"] = _pcol(f("c_ctx"))
            flags[:, 1] = -1.0
            flags[:, 2] = NEG
            m["h0"] = np.zeros((128, depth * 96), np.float32)
            m["kwc"] = np.zeros((depth, 512, 512), np.float32)
            m["knc"] = np.zeros((depth, 768, 512), np.float32)
            m["vwc"] = np.zeros((depth, 512, 256), np.float32)
            m["vnc"] = np.zeros((depth, 512, 768), np.float32)
        m["flags"] = flags
        in_maps.append(m)
    return in_maps


def assemble(results, depth=DEPTH):
    yp = np.zeros((16, 256, D), np.float32)
    ys = np.zeros((4, T, D), np.float32)
    nwk = np.zeros((16, depth, 256, 4, 64), np.float32)
    nwv = np.zeros_like(nwk)
    nnk = np.zeros((16, depth, 256, 12, 64), np.float32)
    nnv = np.zeros_like(nnk)
    sre = np.zeros((16, depth, 2, 48, 64), np.float32)
    sim = np.zeros_like(sre)
    for c in range(8):
        r = results[c]
        y = np.asarray(r["yT"]).T
        if c >= 4:
            ys[c - 4] = y
            continue
        yp[4 * c:4 * c + 4] = y.reshape(4, 256, D)
        kv = np.asarray(r["kv"]).reshape(depth, 4, 256, 2048).transpose(1, 0, 2, 3)
        nwk[4 * c:4 * c + 4] = kv[..., 0:256].reshape(4, depth, 256, 4, 64)
        nwv[4 * c:4 * c + 4] = kv[..., 256:512].reshape(4, depth, 256, 4, 64)
        nnk[4 * c:4 * c + 4] = kv[..., 512:1280].reshape(4, depth, 256, 12, 64)
        nnv[4 * c:4 * c + 4] = kv[..., 1280:2048].reshape(4, depth, 256, 12, 64)
        fin = np.asarray(r["fin"]).reshape(2, 64, depth, 4, 2, 24, 2)
        fin = fin.transpose(3, 2, 4, 5, 0, 1, 6).reshape(4, depth, 2, 48, 64, 2)
        sre[4 * c:4 * c + 4] = fin[..., 0]
        sim[4 * c:4 * c + 4] = fin[..., 1]
    return (yp, ys, nwk, nwv, nnk, nnv, sre, sim)


_NC_CACHE = {}


def kernel(**inputs):
    if DEPTH not in _NC_CACHE:
        _NC_CACHE[DEPTH] = build_nc(DEPTH)
    nc = _NC_CACHE[DEPTH]
    in_maps = prepare(inputs, DEPTH)
    res = run_bass_kernel_spmd(nc, in_maps, core_ids=list(range(8)))
    del in_maps
    return assemble(res.results, DEPTH)
```

```python
import contextlib
import math
import numpy as np
import concourse.bass as bass
import concourse.mybir as mybir
from concourse.ap import AP
from concourse.bass_utils import run_bass_kernel_spmd

F32 = mybir.dt.float32
BF16 = mybir.dt.bfloat16
I32 = mybir.dt.int32
ALU = mybir.AluOpType
AF = mybir.ActivationFunctionType

D = 2048
T = 1024
DEPTH = 4
STOP = 9
NIN = 10496
DFF = 5632
OFF_U, OFF_QW, OFF_KW, OFF_VW, OFF_QN, OFF_KN, OFF_VN, OFF_G = 0, 768, 1536, 1792, 2048, 2816, 3584, 4352
NEG = -30000.0
TWO_PI = 2.0 * math.pi
NA_KB = [list(range(0, 4)), list(range(0, 4)), list(range(0, 5)), list(range(1, 6)),
         list(range(2, 7)), list(range(3, 8)), list(range(4, 8)), list(range(4, 8))]
NA_IDX = {}
for _j in range(8):
    for _b in NA_KB[_j]:
        NA_IDX[(_j, _b)] = len(NA_IDX)
N_NAM = len(NA_IDX)


class Buf:
    __slots__ = ("name", "w", "r", "excl")

    def __init__(self, name="", excl=False):
        self.name = name
        self.w = None
        self.r = {}
        self.excl = excl


class Sched:
    ENG = ("pe", "act", "dve", "pool", "sp")

    def __init__(self, nc, stack, n_dma_sems=32):
        self.nc = nc
        self.e = {"pe": nc.tensor, "act": nc.scalar, "dve": nc.vector, "pool": nc.gpsimd, "sp": nc.sync}
        self.sem = {k: stack.enter_context(nc.semaphore("s_" + k)) for k in self.ENG}
        self.cnt = {k: 0 for k in self.ENG}
        self.seen = {k: {} for k in self.ENG}
        self.dsem = [stack.enter_context(nc.semaphore("d%d" % i)) for i in range(n_dma_sems)]
        self.dcnt = [0] * n_dma_sems
        self.dnext = 0
        self.n_ins = 0
        self.pe_pending = False
        self.inflight = {}
        self.max_inflight = 2

    def _wait(self, eng, kind, key, val):
        if kind == "e" and key == "pe" and eng == "pe":
            return
        if kind == "e" and key == "pe" and val > self.cnt["pe"]:
            assert self.pe_pending and val == self.cnt["pe"] + 1
            self.last_pe.then_inc(self.sem["pe"], 1)
            self.cnt["pe"] += 1
            self.pe_pending = False
        k = (kind, key)
        if self.seen[eng].get(k, 0) >= val:
            return
        sem = self.sem[key] if kind == "e" else self.dsem[key]
        self.e[eng].wait_ge(sem, val)
        self.seen[eng][k] = val

    def _deps(self, eng, reads, writes):
        best = {}
        for b in reads:
            if b.w is not None:
                k = (b.w[0], b.w[1])
                if best.get(k, 0) < b.w[2]:
                    best[k] = b.w[2]
        for b in writes:
            if b.w is not None:
                k = (b.w[0], b.w[1])
                if best.get(k, 0) < b.w[2]:
                    best[k] = b.w[2]
            for k, v in b.r.items():
                if best.get(k, 0) < v:
                    best[k] = v
        for (kind, key), val in best.items():
            self._wait(eng, kind, key, val)

    def _commit(self, tok, reads, writes):
        k = (tok[0], tok[1])
        for b in reads:
            if b.r.get(k, 0) < tok[2]:
                b.r[k] = tok[2]
        for b in writes:
            b.w = tok
            b.r = {}

    def op(self, eng, fn, reads=(), writes=(), inc=True):
        if any(b.excl for b in reads):
            writes = list(writes) + [b for b in reads if b.excl]
            reads = [b for b in reads if not b.excl]
        self._deps(eng, reads, writes)
        ins = fn(self.e[eng])
        if inc:
            self.cnt[eng] += 1
            ins.then_inc(self.sem[eng], 1)
            tok = ("e", eng, self.cnt[eng])
            if eng == "pe":
                self.pe_pending = False
        else:
            assert eng == "pe"
            tok = ("e", eng, self.cnt[eng] + 1)
            self.pe_pending = True
            self.last_pe = ins
        self._commit(tok, reads, writes)
        self.n_ins += 1
        return ins

    def dma(self, eng, out, in_, reads=(), writes=(), **kw):
        i = self.dnext
        self.dnext = (self.dnext + 1) % len(self.dsem)
        if self.dcnt[i] > 0:
            self._wait(eng, "d", i, self.dcnt[i])
        self._deps(eng, reads, writes)
        q = self.inflight.setdefault(eng, [])
        while len(q) >= self.max_inflight:
            t = q.pop(0)
            self._wait(eng, t[0], t[1], t[2])
        ins = self.e[eng].dma_start(out=out, in_=in_, **kw)
        self.dcnt[i] += 16
        ins.then_inc(self.dsem[i], 16)
        tok = ("d", i, self.dcnt[i])
        q.append(tok)
        self._commit(tok, reads, writes)
        self.n_ins += 1
        return tok

    def barrier(self):
        assert not self.pe_pending
        for eng in self.ENG:
            for other in self.ENG:
                if other != eng and self.cnt[other] > 0:
                    self._wait(eng, "e", other, self.cnt[other])
            for i, c in enumerate(self.dcnt):
                if c > 0:
                    self._wait(eng, "d", i, c)


def build_nc(depth=DEPTH):
    nc = bass.Bass("TRN2", target_bir_lowering=False)

    def din(name, shape):
        return nc.dram_tensor(name, list(shape), F32, kind="ExternalInput").ap()

    def dout(name, shape):
        return nc.dram_tensor(name, list(shape), F32, kind="ExternalOutput").ap()

    d_xT = din("xT", [D, T])
    d_cond = din("cond", [128, 16])
    d_flags = din("flags", [128, 4])
    d_h0 = din("h0", [128, depth * 2 * 24 * 2])
    d_kwc = din("kwc", [depth, 512, 512])
    d_knc = din("knc", [depth, 768, 512])
    d_vwc = din("vwc", [depth, 512, 256])
    d_vnc = din("vnc", [depth, 512, 768])
    d_wmask = din("wmask", [128, 24 * 128])
    d_nmask = din("nmask", [128, N_NAM * 128])
    d_nar = din("nar", [depth, 12, 128, 7 * 128])
    d_rope = din("rope", [128, 2 * T])
    d_consts = din("consts", [128, 128 * 3 + 256 + T])
    d_wmod = din("w_mod", [depth, D, 6 * D])
    d_bmod = din("b_mod", [128, depth * 96])
    d_n1g = din("n1g", [128, depth * 16])
    d_n2g = din("n2g", [128, depth * 16])
    d_fg = din("fg", [128, 16])
    d_win = din("w_in", [depth, D, NIN])
    d_wbr = din("w_branch", [depth, 3, 768, D])
    d_wout = din("w_out", [depth, D, D])
    d_wup = din("w_up", [depth, D, 2 * DFF])
    d_wdn = din("w_down", [depth, DFF, D])
    d_wglu = din("w_glu", [depth, 768, 768])
    d_bglu = din("b_glu", [128, depth * 6])
    d_convw = din("conv_w", [128, depth * 3 * 88])
    d_convb = din("conv_b", [128, depth * 88])
    d_ssmd = din("ssm_d", [128, depth * 6])
    d_sink = din("sink", [128, depth * 12])
    d_lamre = din("lam_re", [128, depth * 48])
    d_lamim = din("lam_im", [128, depth * 48])
    d_logdt = din("log_dt", [128, depth * 48])
    d_bmat = din("bmat", [depth, 24, 128, 4 * 128])
    d_cmat = din("cmat", [depth, 24, 128, 4 * 128])
    o_yT = dout("yT", [D, T])
    o_kv = dout("kv", [depth, T, 2048])
    o_fin = dout("fin", [128, depth * 4 * 2 * 24 * 2])

    st = contextlib.ExitStack()
    with st:
        S = Sched(nc, st)

        uid = [0]

        def sb(name, shape, dt, stack=st):
            uid[0] += 1
            return stack.enter_context(nc.sbuf_tensor("%s_%d" % (name, uid[0]), list(shape), dt))

        XT = sb("XT", [128, 16, T], F32)
        bXT = [Buf("XT%d" % i) for i in range(16)]
        HT = sb("HT", [128, 16, T], BF16)
        bHT = [Buf("HT%d" % i) for i in range(16)]
        NSLOT = 2
        WSL = [sb("wsl%d" % i, [128, 4096], BF16) for i in range(NSLOT)]
        bWSL = [Buf("wsl%d" % i) for i in range(NSLOT)]
        wnext = [0]
        ident = sb("ident", [128, 128], BF16)
        pswap = sb("pswap", [128, 128], BF16)
        ones_bf = sb("ones_bf", [128, 128], BF16)
        small = sb("small", [128, 1152], F32)
        bC = Buf("consts")
        bSmall = Buf("small")
        PS = [st.enter_context(nc.psum_tensor("ps%d" % i, [128, 1024], F32)) for i in range(4)]
        bPS = [[Buf("ps%d_%d" % (i, h), excl=True) for h in range(2)] for i in range(4)]

        col = {}
        cpos = [0]

        def scol(name, n):
            col[name] = cpos[0]
            cpos[0] += n
            assert cpos[0] <= 1152
            return col[name]

        for nm, n in [("cond", 16), ("flags", 4), ("bmod", depth * 96), ("n1g", depth * 16), ("n2g", depth * 16),
                      ("fg", 16), ("bglu", depth * 6), ("ssmd", depth * 6), ("sink", depth * 12)]:
            scol(nm, n)
        for nm in ["modv0", "modv1", "A10", "A11", "A20", "A21"]:
            scol(nm, 96 if nm.startswith("modv") else 16)
        scol("nw0", 88), scol("nw2", 88), scol("sinke", 12), scol("mhpi", 1)

        def SM(name, a=0, n=1):
            return small[:, col[name] + a: col[name] + a + n]

        big2 = sb("big2", [128, depth * 3 * 88 + depth * 88], F32)
        CW0 = 0
        CB0 = depth * 3 * 88
        ssmp = sb("ssmp", [128, 3 * depth * 48 + 2 * depth * 96], F32)
        fin = sb("fin", [128, 4 * 2 * 24 * 2], F32)
        bFin = Buf("fin")

        def ld(dst, src, eng="sp"):
            S.dma(eng, dst, src, writes=[bC])

        ld(small[:, col["cond"]:col["cond"] + 16], d_cond[:, :])
        ld(small[:, col["flags"]:col["flags"] + 4], d_flags[:, :])
        ld(small[:, col["bmod"]:col["bmod"] + depth * 96], d_bmod[:, :])
        ld(small[:, col["n1g"]:col["n1g"] + depth * 16], d_n1g[:, :])
        ld(small[:, col["n2g"]:col["n2g"] + depth * 16], d_n2g[:, :])
        ld(small[:, col["fg"]:col["fg"] + 16], d_fg[:, :])
        ld(small[:, col["bglu"]:col["bglu"] + depth * 6], d_bglu[:, :])
        ld(small[:, col["ssmd"]:col["ssmd"] + depth * 6], d_ssmd[:, :])
        ld(small[:, col["sink"]:col["sink"] + depth * 12], d_sink[:, :])
        ld(big2[:, CW0:CW0 + depth * 264], d_convw[:, :])
        ld(big2[:, CB0:CB0 + depth * 88], d_convb[:, :])
        ld(ssmp[:, 0:depth * 48], d_lamre[:, :])
        ld(ssmp[:, depth * 48:2 * depth * 48], d_lamim[:, :])
        ld(ssmp[:, 2 * depth * 48:3 * depth * 48], d_logdt[:, :])
        ld(ssmp[:, 3 * depth * 48:3 * depth * 48 + depth * 96], d_h0[:, :])
        ld(ident[:], d_consts[:, 0:128], "pool")
        ld(pswap[:], d_consts[:, 128:256], "pool")
        ld(ones_bf[:], d_consts[:, 256:384], "pool")
        for kt in range(16):
            S.dma("sp", XT[:, kt, :], d_xT[kt * 128:(kt + 1) * 128, :], writes=[bXT[kt]])
        S.op("pool", lambda e: e.memset(small[:, col["mhpi"]:col["mhpi"] + 1], -math.pi / 2), writes=[bC])
        S.barrier()

        def wslot():
            i = wnext[0] % len(WSL)
            wnext[0] = (i + 1) % len(WSL)
            return WSL[i], bWSL[i]

        @contextlib.contextmanager
        def extra_slots(stk, n):
            base = len(WSL)
            for i in range(n):
                WSL.append(sb("wslx%d" % i, [128, 4096], BF16, stk))
                bWSL.append(Buf("wslx%d" % i))
            try:
                yield
            finally:
                del WSL[base:]
                del bWSL[base:]
                wnext[0] = 0

        def load_w(segs, nk, width):
            slot, b = wslot()
            view = slot[:, 0:nk * width].rearrange("p (k c) -> p k c", k=nk)
            for src, c0 in segs:
                w = src.shape[1]
                S.dma("pool", view[:, :, c0:c0 + w], src.rearrange("(k p) c -> p k c", p=128), writes=[b])
            return view, b

        def mm(out, lhsT, rhs, start, stop, reads, writes, inc=None):
            if inc is None:
                inc = stop
            S.op("pe", lambda e: e.matmul(out, lhsT=lhsT, rhs=rhs, start=start, stop=stop),
                 reads=reads, writes=writes, inc=inc)

        def psh(i, h):
            return PS[i][:, h * 512:(h + 1) * 512]

        def rev(ap2d):
            (ps_, pc), (fs, fc) = ap2d.ap
            last = ap2d[:, fc - 1:fc]
            return AP(ap2d.tensor, last.offset, [[ps_, pc], [-fs, fc]])

        def chunked(ap2d, n, reverse=False):
            (ps_, pc), (fs, fc) = ap2d.ap
            if reverse:
                last = ap2d[:, fc - 1:fc]
                return AP(ap2d.tensor, last.offset, [[ps_, pc], [0, n], [-fs, fc]])
            return AP(ap2d.tensor, ap2d.offset, [[ps_, pc], [0, n], [fs, fc]])

        scb = sb("scb", [128, 16], BF16)
        bScb = Buf("scb")
        S.op("act", lambda e: e.activation(out=scb[:], in_=SM("cond", 0, 16), func=AF.Silu), reads=[bC], writes=[bScb])

        def phase_mod_gen(l):
            pb = bPS[3][0]
            mv, a1, a2 = "modv%d" % (l % 2), "A1%d" % (l % 2), "A2%d" % (l % 2)
            for pi in range(48):
                wp, wb = load_w([(d_wmod[l][:, pi * 256:(pi + 1) * 256], 0)], 16, 256)
                for j in range(2):
                    mt = pi * 2 + j
                    for kt in range(16):
                        mm(PS[3][:, mt:mt + 1], wp[:, kt, j * 128:(j + 1) * 128], scb[:, kt:kt + 1],
                           kt == 0, kt == 15, [wb, bScb], [pb])
                yield
            S.op("dve", lambda e: e.tensor_tensor(out=SM(mv, 0, 96), in0=PS[3][:, 0:96],
                                                  in1=SM("bmod", l * 96, 96), op=ALU.add),
                 reads=[pb, bC], writes=[bSmall])
            S.op("dve", lambda e: e.scalar_tensor_tensor(out=SM(a1, 0, 16), in0=SM(mv, 16, 16), scalar=1.0,
                                                         in1=SM("n1g", l * 16, 16), op0=ALU.add, op1=ALU.mult),
                 reads=[bSmall, bC], writes=[bSmall])
            S.op("dve", lambda e: e.scalar_tensor_tensor(out=SM(a2, 0, 16), in0=SM(mv, 64, 16), scalar=1.0,
                                                         in1=SM("n2g", l * 16, 16), op0=ALU.add, op1=ALU.mult),
                 reads=[bSmall, bC], writes=[bSmall])
            yield

        def phase_mod(l):
            for _ in phase_mod_gen(l):
                pass

        def phase_norm(Aname, Aoff, Bname, Boff, stk, final=False):
            sq = [sb("sq%d" % i, [128, T], BF16, stk) for i in range(2)]
            bsq = [Buf(), Buf()]
            rstd = sb("rstd", [128, T], F32, stk)
            brs = Buf()
            tmp = [sb("ntmp%d" % i, [128, T], F32, stk) for i in range(2)]
            btmp = [Buf(), Buf()]
            for kt in range(16):
                i = kt % 2
                S.op("act", lambda e: e.activation(out=sq[i][:], in_=XT[:, kt, :], func=AF.Square),
                     reads=[bXT[kt]], writes=[bsq[i]])
                for hf in range(2):
                    mm(psh(3, hf), ones_bf[:], sq[i][:, hf * 512:(hf + 1) * 512], kt == 0, kt == 15,
                       [bsq[i], bC], [bPS[3][hf]])
            S.op("dve", lambda e: e.tensor_scalar(out=rstd[:], in0=PS[3][:, :], scalar1=1.0 / D, scalar2=1e-6,
                                                  op0=ALU.mult, op1=ALU.add), reads=bPS[3], writes=[brs])
            S.op("act", lambda e: e.activation(out=rstd[:], in_=rstd[:], func=AF.Sqrt), reads=[brs], writes=[brs])
            S.op("dve", lambda e: e.reciprocal(out=rstd[:], in_=rstd[:]), reads=[brs], writes=[brs])
            for kt in range(16):
                i = kt % 2
                S.op("dve", lambda e: e.scalar_tensor_tensor(out=tmp[i][:], in0=XT[:, kt, :],
                                                             scalar=SM(Aname, Aoff + kt, 1), in1=rstd[:],
                                                             op0=ALU.mult, op1=ALU.mult),
                     reads=[bXT[kt], brs, bSmall, bC], writes=[btmp[i]])
                if final:
                    S.dma("sp", o_yT[kt * 128:(kt + 1) * 128, :], tmp[i][:], reads=[btmp[i]], writes=[bOut])
                else:
                    S.op("act", lambda e: e.activation(out=HT[:, kt, :], in_=tmp[i][:], func=AF.Identity,
                                                       bias=SM(Bname, Boff + kt, 1)),
                         reads=[btmp[i], bSmall], writes=[bHT[kt]])

        bOut = Buf("out")

        pcount = [0]

        def proj_fm(l, colsegs_per_mtile, evac):
            n = len(colsegs_per_mtile)
            for m0 in range(0, n, 2):
                grp = colsegs_per_mtile[m0:m0 + 2]
                segs = []
                c = 0
                for cs in grp:
                    for (c0, w) in cs:
                        segs.append((d_win[l][:, c0:c0 + w], c))
                        c += w
                wp, wb = load_w(segs, 16, 128 * len(grp))
                for j in range(len(grp)):
                    for hf in range(2):
                        bank = pcount[0] % 2
                        pcount[0] += 1
                        for kt in range(16):
                            mm(psh(bank, hf), wp[:, kt, j * 128:(j + 1) * 128], HT[:, kt, hf * 512:(hf + 1) * 512],
                               kt == 0, kt == 15, [wb, bHT[kt]], [bPS[bank][hf]])
                        evac(m0 + j, hf, psh(bank, hf), bPS[bank][hf])

        def tokmajor(l, c0, w, kvcol, vdst, stgs):
            stg, bst = stgs
            wp, wb = load_w([(d_win[l][:, c0:c0 + w], 0)], 16, w)
            for tb in range(8):
                bank = pcount[0] % 2
                pcount[0] += 1
                for kt in range(16):
                    mm(PS[bank][:, 0:w], HT[:, kt, tb * 128:(tb + 1) * 128], wp[:, kt, :], kt == 0, kt == 15,
                       [wb, bHT[kt]], [bPS[bank][0]])
                i = tb % 2
                S.op("act", lambda e: e.activation(out=stg[i][:, 0:w], in_=PS[bank][:, 0:w], func=AF.Copy),
                     reads=[bPS[bank][0]], writes=[bst[i]])
                S.dma("sp", o_kv[l][tb * 128:(tb + 1) * 128, kvcol:kvcol + w], stg[i][:, 0:w], reads=[bst[i]],
                      writes=[bOut])
                if vdst is not None:
                    vt, vb, voff = vdst
                    S.op("dve", lambda e: e.tensor_copy(out=vt[:, tb, voff:voff + w], in_=PS[bank][:, 0:w]),
                         reads=[bPS[bank][0]], writes=[vb])

        def attn_head(qtile, bq, qt, qh, ktile, bk, ktidx, kctx, bkc, kcidx, vaug_loc, vaug_ctx, bv, bvc,
                      local_units, sink_ap, stk_bufs):
            (pt, bpt, rinv, brinv) = stk_bufs
            pr = slice(qh * 64, qh * 64 + 64)
            bqo = Buf("qout")

            def stage_a(n):
                qs = slice(n * 128, (n + 1) * 128)
                sa, sb_ = ((2, 3), (0, 1))[n % 2]
                q_ap = qtile[pr, qt, qs]
                units = local_units(n)
                nl = len(units)
                for u, (b, biases) in enumerate(units):
                    hf = u // 4
                    o_ap = PS[sa][:, u * 128:(u + 1) * 128]
                    mm(o_ap, ktile[pr, ktidx, b * 128:(b + 1) * 128], q_ap, True, len(biases) == 0,
                       [bk, bq], [bPS[sa][hf]], inc=(len(biases) == 0))
                    for bi, bias_ap in enumerate(biases):
                        mm(o_ap, ident[:], bias_ap, False, bi == len(biases) - 1, [bC], [bPS[sa][hf]])
                for cb in range(4):
                    mm(PS[sb_][:, cb * 128:(cb + 1) * 128], kctx[pr, kcidx, cb * 128:(cb + 1) * 128], q_ap, True, True,
                       [bkc, bq], [bPS[sb_][0]])
                i = n % 2
                S.op("act", lambda e: e.activation(out=pt[i][:, 0:nl * 128], in_=PS[sa][:, 0:nl * 128], func=AF.Exp),
                     reads=bPS[sa], writes=[bpt[i]])
                S.op("act", lambda e: e.activation(out=pt[i][:, 640:1152], in_=PS[sb_][:, 0:512], func=AF.Exp,
                                                   bias=SM("flags", 2, 1)),
                     reads=[bPS[sb_][0], bC], writes=[bpt[i]])

            def stage_b(n):
                qs = slice(n * 128, (n + 1) * 128)
                sa, sb_ = ((2, 3), (0, 1))[n % 2]
                units = local_units(n)
                i = n % 2
                for u, (b, _) in enumerate(units):
                    mm(PS[sb_][0:64, 512:640], vaug_loc(b), pt[i][:, u * 128:(u + 1) * 128], u == 0, False, [bv, bpt[i]],
                       [bPS[sb_][1]], inc=False)
                    mm(PS[sb_][64:128, 512:640], ones_bf[:, 0:64], pt[i][:, u * 128:(u + 1) * 128], u == 0, False,
                       [bC, bpt[i]], [bPS[sb_][1]], inc=False)
                for cb in range(4):
                    mm(PS[sb_][0:64, 512:640], vaug_ctx(cb), pt[i][:, 640 + cb * 128:640 + (cb + 1) * 128], False, cb == 3,
                       [bvc, bpt[i]], [bPS[sb_][1]], inc=False)
                    mm(PS[sb_][64:128, 512:640], ones_bf[:, 0:64], pt[i][:, 640 + cb * 128:640 + (cb + 1) * 128], False, cb == 3,
                       [bC, bpt[i]], [bPS[sb_][1]], inc=(cb == 3))
                if sink_ap is not None:
                    S.op("dve", lambda e: e.tensor_scalar(out=rinv[i][64:128, :], in0=PS[sb_][64:128, 512:640],
                                                          scalar1=sink_ap[64:128, :], scalar2=None, op0=ALU.add),
                         reads=[bPS[sb_][1], bSmall], writes=[brinv[i]])
                    S.op("dve", lambda e: e.reciprocal(out=rinv[i][64:128, :], in_=rinv[i][64:128, :]),
                         reads=[brinv[i]], writes=[brinv[i]])
                else:
                    S.op("dve", lambda e: e.reciprocal(out=rinv[i][64:128, :], in_=PS[sb_][64:128, 512:640]),
                         reads=[bPS[sb_][1]], writes=[brinv[i]])
                S.op("dve", lambda e: e.tensor_tensor(out=qtile[pr, qt, qs], in0=PS[sb_][0:64, 512:640],
                                                      in1=rinv[i][64:128, :], op=ALU.mult),
                     reads=[bPS[sb_][1], brinv[i]], writes=[bqo])

            extra = attn_extra[0]

            def wrap(fn):
                def g(n):
                    old = attn_extra[0]
                    attn_extra[0] = extra
                    fn(n)
                    attn_extra[0] = old
                return g
            return wrap(stage_a), wrap(stage_b)

        def run_pipeline(makers):
            pending = None
            for mk in makers:
                a, b = mk()
                for n in range(8):
                    a(n)
                    if pending is not None:
                        pending()
                    pending = (lambda b=b, n=n: b(n))
            if pending is not None:
                pending()

        def vaug_ap(vt, base, ones_off):
            return vt[:, base:base + 64]

        def layer(l):
            MV, A1N, A2N = "modv%d" % (l % 2), "A1%d" % (l % 2), "A2%d" % (l % 2)
            if l == 0 or STOP < 3:
                phase_mod(l)
            with contextlib.ExitStack() as stk:
                phase_norm(A1N, 0, MV, 0, stk)
                S.barrier()
            if STOP <= 1:
                return
            with contextlib.ExitStack() as mx:
                OS = sb("OS", [128, 6, T], BF16, mx)
                OW = sb("OW", [128, 6, T], BF16, mx)
                ON = sb("ON", [128, 6, T], BF16, mx)
                bOS = [Buf() for _ in range(6)]
                bOW = [Buf() for _ in range(6)]
                bON = [Buf() for _ in range(6)]

                with contextlib.ExitStack() as sk:
                    def ev_u(mt, hf, ps, pb):
                        S.op("act", lambda e: e.activation(out=OS[:, mt, hf * 512:(hf + 1) * 512], in_=ps, func=AF.Copy),
                             reads=[pb], writes=[bOS[mt]])
                    proj_fm(l, [[(OFF_U + m * 128, 128)] for m in range(6)], ev_u)
                    if STOP >= 3:
                        ssm_branch(l, OS, bOS, ON, bON, sk)
                    S.barrier()
                with contextlib.ExitStack() as wk:
                    if STOP >= 4:
                        win_branch(l, OW, bOW, wk)
                    S.barrier()
                for hg in range(2):
                    with contextlib.ExitStack() as nk:
                        if STOP >= 5:
                            na_branch(l, hg, ON, bON, nk)
                        S.barrier()
                with contextlib.ExitStack() as gk:
                    if STOP >= 6:
                        with extra_slots(gk, 2):
                            merge(l, OS, bOS, OW, bOW, ON, bON, gk)
                            S.barrier()
                    S.barrier()
            if STOP <= 6:
                return
            with contextlib.ExitStack() as stk:
                phase_norm(A2N, 0, MV, 48, stk)
                S.barrier()
            with contextlib.ExitStack() as fk:
                with extra_slots(fk, 3):
                    ffn(l, fk)
                    S.barrier()

        def ssm_branch(l, OS, bOS, YS, bYS, sk):
            P48 = depth * 48
            lam_re = ssmp[:, l * 48:(l + 1) * 48]
            lam_im = ssmp[:, P48 + l * 48:P48 + (l + 1) * 48]
            logdt = ssmp[:, 2 * P48 + l * 48:2 * P48 + (l + 1) * 48]
            H0 = 3 * P48 + l * 96
            ttab = sb("ttab", [128, 256], F32, sk)
            rmask = sb("rmask", [128, 512], F32, sk)
            S.dma("sp", ttab[:], d_consts[:, 384:640], writes=[bC])
            S.dma("sp", rmask[:], d_consts[:, 640:640 + 512], writes=[bC])
            sp_ = sb("sp_", [128, 13 * 48], F32, sk)
            bsp = Buf()
            ki = sb("ssm_ki", [128, 256], I32, sk)

            def P(i):
                return sp_[:, i * 48:(i + 1) * 48]
            A_, NA_, TH, MAG, SN, CS, FR, FI, T0, T1, T2, DEN, THN = [P(i) for i in range(13)]

            def dv(fn, r=(), w=()):
                S.op("dve", fn, reads=list(r) + [bsp], writes=list(w) + [bsp])

            def sincos(th_ap, out_s, out_c, x_t, k_t):
                for shift, dst in ((0.0, out_s), (math.pi / 2, out_c)):
                    dv(lambda e: e.tensor_scalar(out=k_t, in0=th_ap, scalar1=shift, scalar2=1.0 / TWO_PI,
                                                 op0=ALU.add, op1=ALU.mult))
                    dv(lambda e: e.tensor_copy(out=x_t, in_=k_t))
                    dv(lambda e: e.scalar_tensor_tensor(out=x_t, in0=x_t, scalar=-TWO_PI, in1=th_ap,
                                                        op0=ALU.mult, op1=ALU.add))
                    dv(lambda e: e.tensor_scalar(out=x_t, in0=x_t, scalar1=shift, scalar2=math.pi,
                                                 op0=ALU.add, op1=ALU.min))
                    dv(lambda e: e.tensor_scalar(out=x_t, in0=x_t, scalar1=-math.pi, scalar2=None, op0=ALU.max))
                    S.op("act", lambda e: e.activation(out=dst, in_=x_t, func=AF.Sin), reads=[bsp], writes=[bsp])

            S.op("act", lambda e: e.activation(out=T0, in_=logdt, func=AF.Exp), reads=[bC], writes=[bsp])
            dv(lambda e: e.tensor_tensor(out=A_, in0=T0, in1=lam_re, op=ALU.mult), r=[bC])
            dv(lambda e: e.tensor_tensor(out=TH, in0=T0, in1=lam_im, op=ALU.mult), r=[bC])
            dv(lambda e: e.tensor_scalar(out=NA_, in0=A_, scalar1=-1.0, scalar2=None, op0=ALU.mult))
            dv(lambda e: e.tensor_scalar(out=THN, in0=TH, scalar1=1.0 / TWO_PI, scalar2=None, op0=ALU.mult))
            S.op("act", lambda e: e.activation(out=MAG, in_=A_, func=AF.Exp), reads=[bsp], writes=[bsp])
            sincos(TH, SN, CS, T1, ki[:, 0:48])
            dv(lambda e: e.tensor_tensor(out=T0, in0=MAG, in1=CS, op=ALU.mult))
            dv(lambda e: e.tensor_scalar(out=T0, in0=T0, scalar1=-1.0, scalar2=None, op0=ALU.add))
            dv(lambda e: e.tensor_tensor(out=T1, in0=MAG, in1=SN, op=ALU.mult))
            dv(lambda e: e.tensor_tensor(out=DEN, in0=lam_re, in1=lam_re, op=ALU.mult), r=[bC])
            dv(lambda e: e.tensor_tensor(out=T2, in0=lam_im, in1=lam_im, op=ALU.mult), r=[bC])
            dv(lambda e: e.tensor_tensor(out=DEN, in0=DEN, in1=T2, op=ALU.add))
            dv(lambda e: e.reciprocal(out=DEN, in_=DEN))
            dv(lambda e: e.tensor_tensor(out=FR, in0=T0, in1=lam_re, op=ALU.mult), r=[bC])
            dv(lambda e: e.tensor_tensor(out=T2, in0=T1, in1=lam_im, op=ALU.mult), r=[bC])
            dv(lambda e: e.tensor_tensor(out=FR, in0=FR, in1=T2, op=ALU.add))
            dv(lambda e: e.tensor_tensor(out=FR, in0=FR, in1=DEN, op=ALU.mult))
            dv(lambda e: e.tensor_tensor(out=FI, in0=T1, in1=lam_re, op=ALU.mult), r=[bC])
            dv(lambda e: e.tensor_tensor(out=T2, in0=T0, in1=lam_im, op=ALU.mult), r=[bC])
            dv(lambda e: e.tensor_tensor(out=FI, in0=FI, in1=T2, op=ALU.subtract))
            dv(lambda e: e.tensor_tensor(out=FI, in0=FI, in1=DEN, op=ALU.mult))

            tb_ = sb("ssm_tab", [128, 6 * 256], F32, sk)
            bt = Buf()
            EPr, EPi, EMr, EMi = [sb("ssm_E%d" % i, [128, 256], F32, sk) for i in range(4)]
            bE = Buf()
            W1, W2, ZR, ZI = [sb("ssm_w%d" % i, [128, 512], F32, sk) for i in range(4)]
            bW1, bW2, bZR, bZI = Buf(), Buf(), Buf(), Buf()
            XR = sb("ssm_xr", [128, 512], BF16, sk)
            XI = sb("ssm_xi", [128, 512], BF16, sk)
            bX = Buf()
            pc_ = sb("ssm_pc", [128, 32], F32, sk)
            bpc = Buf()
            ytmp = sb("ssm_yt", [128, 512], F32, sk)
            byt = Buf()
            bm_t = [sb("ssm_bm%d" % i, [128, 4 * 128], BF16, sk) for i in range(2)]
            cm_t = [sb("ssm_cm%d" % i, [128, 4 * 128], BF16, sk) for i in range(2)]
            bbm = [Buf(), Buf()]

            def TB(i):
                return tb_[:, i * 256:(i + 1) * 256]

            def v2(a):
                return a.rearrange("p (c t) -> p c t", c=2)

            modgen = phase_mod_gen(l + 1) if (l + 1 < depth) else iter(())
            for kt in range(6):
                for g4 in range(4):
                    gp = kt * 4 + g4
                    sl = gp % 2
                    next(modgen, None)
                    next(modgen, None)
                    S.dma("pool", bm_t[sl][:], d_bmat[l][gp], writes=[bbm[sl]])
                    S.dma("pool", cm_t[sl][:], d_cmat[l][gp], writes=[bbm[sl]])
                    for d_ in range(2):
                        c = d_ * 24 + gp
                        r_ = (d_ == 1)
                        S.op("act", lambda e: e.activation(out=TB(4), in_=ttab[:], func=AF.Exp, scale=A_[:, c:c + 1]),
                             reads=[bsp, bC], writes=[bt])
                        S.op("act", lambda e: e.activation(out=TB(5), in_=ttab[:], func=AF.Exp, scale=NA_[:, c:c + 1]),
                             reads=[bsp, bC], writes=[bt])
                        S.op("act", lambda e: e.activation(out=TB(0), in_=ttab[:], func=AF.Identity, scale=TH[:, c:c + 1]),
                             reads=[bsp, bC], writes=[bt])
                        S.op("dve", lambda e: e.tensor_scalar(out=ki[:], in0=ttab[:], scalar1=THN[:, c:c + 1],
                                                              scalar2=None, op0=ALU.mult), reads=[bsp, bC, bt], writes=[bt])
                        S.op("dve", lambda e: e.tensor_copy(out=TB(1), in_=ki[:]), reads=[bt], writes=[bt])
                        S.op("dve", lambda e: e.scalar_tensor_tensor(out=TB(1), in0=TB(1), scalar=-TWO_PI, in1=TB(0),
                                                                     op0=ALU.mult, op1=ALU.add), reads=[bt], writes=[bt])
                        S.op("dve", lambda e: e.tensor_scalar(out=TB(1), in0=TB(1), scalar1=math.pi, scalar2=-math.pi,
                                                              op0=ALU.min, op1=ALU.max), reads=[bt], writes=[bt])
                        S.op("act", lambda e: e.activation(out=TB(2), in_=TB(1), func=AF.Sin), reads=[bt], writes=[bt])
                        S.op("dve", lambda e: e.scalar_tensor_tensor(out=TB(1), in0=TB(1), scalar=-1.0, in1=TB(1),
                                                                     op0=ALU.mult, op1=ALU.max), reads=[bt], writes=[bt])
                        S.op("act", lambda e: e.activation(out=TB(3), in_=TB(1), func=AF.Sin, bias=SM("mhpi", 0, 1)),
                             reads=[bt, bC], writes=[bt])
                        S.op("dve", lambda e: e.scalar_tensor_tensor(out=EPr[:], in0=TB(4), scalar=-1.0, in1=TB(3),
                                                                     op0=ALU.mult, op1=ALU.mult), reads=[bt, bE], writes=[bE])
                        S.op("dve", lambda e: e.tensor_tensor(out=EPi[:], in0=TB(4), in1=TB(2), op=ALU.mult), reads=[bt, bE], writes=[bE])
                        S.op("dve", lambda e: e.scalar_tensor_tensor(out=TB(4), in0=TB(5), scalar=-1.0, in1=TB(3),
                                                                     op0=ALU.mult, op1=ALU.mult), reads=[bt], writes=[bt])
                        S.op("dve", lambda e: e.tensor_tensor(out=TB(5), in0=TB(5), in1=TB(2), op=ALU.mult), reads=[bt], writes=[bt])
                        S.op("act", lambda e: e.activation(out=EMr[:], in_=TB(4), func=AF.Identity, scale=FR[:, c:c + 1]),
                             reads=[bt, bsp, bE], writes=[bE])
                        S.op("dve", lambda e: e.scalar_tensor_tensor(out=EMr[:], in0=TB(5), scalar=FI[:, c:c + 1], in1=EMr[:],
                                                                     op0=ALU.mult, op1=ALU.add), reads=[bt, bsp, bE], writes=[bE])
                        S.op("act", lambda e: e.activation(out=TB(1), in_=TB(5), func=AF.Identity, scale=FR[:, c:c + 1]),
                             reads=[bt, bsp], writes=[bt])
                        S.op("dve", lambda e: e.scalar_tensor_tensor(out=EMi[:], in0=TB(4), scalar=FI[:, c:c + 1], in1=TB(1),
                                                                     op0=ALU.mult, op1=ALU.subtract), reads=[bt, bsp, bE], writes=[bE])
                        for hf in range(2):
                            hs = slice(hf * 512, (hf + 1) * 512)
                            mm(psh(0, hf), bm_t[sl][:, (d_ * 2) * 128:(d_ * 2 + 1) * 128], OS[:, kt, hs],
                               True, True, [bbm[sl], bOS[kt]], [bPS[0][hf]])
                            mm(psh(1, hf), bm_t[sl][:, (d_ * 2 + 1) * 128:(d_ * 2 + 2) * 128], OS[:, kt, hs],
                               True, True, [bbm[sl], bOS[kt]], [bPS[1][hf]])
                        S.op("dve", lambda e: e.tensor_scalar(out=pc_[:, 6:7], in0=EPi[:, 255:256], scalar1=-1.0,
                                                              scalar2=None, op0=ALU.mult), reads=[bE, bpc], writes=[bpc])
                        S.op("dve", lambda e: e.tensor_copy(out=pc_[:, 7:8], in_=EPi[:, 255:256]), reads=[bE, bpc], writes=[bpc])
                        h0c = H0 + (d_ * 24 + gp) * 2
                        first_chunk = 0 if not r_ else 3
                        S.op("dve", lambda e: e.tensor_copy(out=pc_[:, 8 + 2 * first_chunk:10 + 2 * first_chunk],
                                                            in_=ssmp[:, h0c:h0c + 2]), reads=[bC, bpc], writes=[bpc])
                        fw = (lambda a: a) if not r_ else rev
                        for hf in ([0, 1] if not r_ else [1, 0]):
                            hs = slice(hf * 512, (hf + 1) * 512)
                            bur, bui = v2(psh(0, hf)), v2(psh(1, hf))
                            er, ei = chunked(EMr[:, :], 2, r_), chunked(EMi[:, :], 2, r_)
                            S.op("dve", lambda e: e.tensor_tensor(out=v2(W1[:, :]), in0=bur, in1=er, op=ALU.mult), reads=[bPS[0][hf], bE, bW1], writes=[bW1])
                            S.op("dve", lambda e: e.tensor_tensor(out=v2(W2[:, :]), in0=bui, in1=ei, op=ALU.mult), reads=[bPS[1][hf], bE, bW2], writes=[bW2])
                            S.op("dve", lambda e: e.tensor_tensor(out=W1[:], in0=W1[:], in1=W2[:], op=ALU.subtract), reads=[bW1, bW2], writes=[bW1])
                            S.op("dve", lambda e: e.tensor_tensor(out=v2(ZR[:, :]), in0=bui, in1=er, op=ALU.mult), reads=[bPS[1][hf], bE, bZR], writes=[bZR])
                            S.op("dve", lambda e: e.tensor_tensor(out=v2(W2[:, :]), in0=bur, in1=ei, op=ALU.mult), reads=[bPS[0][hf], bE, bW2], writes=[bW2])
                            S.op("dve", lambda e: e.tensor_tensor(out=ZR[:], in0=ZR[:], in1=W2[:], op=ALU.add), reads=[bZR, bW2], writes=[bZR])
                            S.op("dve", lambda e: e.tensor_tensor_scan(out=fw(W2[:, :]), data0=rmask[:, :], data1=fw(W1[:, :]),
                                                                       initial=0.0, op0=ALU.mult, op1=ALU.add),
                                 reads=[bW1, bC, bW2], writes=[bW2])
                            S.op("dve", lambda e: e.tensor_tensor_scan(out=fw(ZI[:, :]), data0=rmask[:, :], data1=fw(ZR[:, :]),
                                                                       initial=0.0, op0=ALU.mult, op1=ALU.add),
                                 reads=[bZR, bC, bZI], writes=[bZI])
                            chunks = [2 * hf, 2 * hf + 1] if not r_ else [2 * hf + 1, 2 * hf]
                            for cch in chunks:
                                cl = cch - 2 * hf
                                endcol = cl * 256 + (255 if not r_ else 0)
                                pcol = 8 + 2 * cch
                                S.op("dve", lambda e: e.tensor_tensor(out=pc_[:, 2:3], in0=W2[:, endcol:endcol + 1],
                                                                      in1=pc_[:, pcol:pcol + 1], op=ALU.add), reads=[bW2, bpc], writes=[bpc])
                                S.op("dve", lambda e: e.tensor_tensor(out=pc_[:, 3:4], in0=ZI[:, endcol:endcol + 1],
                                                                      in1=pc_[:, pcol + 1:pcol + 2], op=ALU.add), reads=[bZI, bpc], writes=[bpc])
                                S.op("dve", lambda e: e.tensor_tensor(out=pc_[:, 4:6], in0=rev(pc_[:, 2:4]), in1=pc_[:, 6:8],
                                                                      op=ALU.mult), reads=[bpc], writes=[bpc])
                                fcol = ((cch * 2 + d_) * 24 + gp) * 2
                                S.op("dve", lambda e: e.scalar_tensor_tensor(out=fin[:, fcol:fcol + 2], in0=pc_[:, 2:4],
                                                                             scalar=EPr[:, 255:256], in1=pc_[:, 4:6],
                                                                             op0=ALU.mult, op1=ALU.add),
                                     reads=[bpc, bE, bFin], writes=[bFin])
                                nxt = cch + (1 if not r_ else -1)
                                if 0 <= nxt < 4:
                                    ncol = 8 + 2 * nxt
                                    S.op("dve", lambda e: e.tensor_scalar(out=pc_[:, ncol:ncol + 2], in0=fin[:, fcol:fcol + 2],
                                                                          scalar1=SM("flags", 0, 1), scalar2=None, op0=ALU.mult),
                                         reads=[bFin, bC, bpc], writes=[bpc])
                            for cl in range(2):
                                pcol = 8 + 2 * (2 * hf + cl)
                                cs = slice(cl * 256, (cl + 1) * 256)
                                S.op("act", lambda e: e.activation(out=W2[:, cs], in_=W2[:, cs], func=AF.Identity,
                                                                   bias=pc_[:, pcol:pcol + 1]), reads=[bW2, bpc], writes=[bW2])
                                S.op("act", lambda e: e.activation(out=ZI[:, cs], in_=ZI[:, cs], func=AF.Identity,
                                                                   bias=pc_[:, pcol + 1:pcol + 2]), reads=[bZI, bpc], writes=[bZI])
                            er, ei = chunked(EPr[:, :], 2, r_), chunked(EPi[:, :], 2, r_)
                            S.op("dve", lambda e: e.tensor_tensor(out=v2(W1[:, :]), in0=v2(W2[:, :]), in1=er, op=ALU.mult), reads=[bW2, bE, bW1], writes=[bW1])
                            S.op("dve", lambda e: e.tensor_tensor(out=v2(ZR[:, :]), in0=v2(ZI[:, :]), in1=ei, op=ALU.mult), reads=[bZI, bE, bZR], writes=[bZR])
                            S.op("dve", lambda e: e.tensor_tensor(out=XR[:], in0=W1[:], in1=ZR[:], op=ALU.subtract), reads=[bW1, bZR, bX], writes=[bX])
                            S.op("dve", lambda e: e.tensor_tensor(out=v2(W1[:, :]), in0=v2(ZI[:, :]), in1=er, op=ALU.mult), reads=[bZI, bE, bW1], writes=[bW1])
                            S.op("dve", lambda e: e.tensor_tensor(out=v2(ZR[:, :]), in0=v2(W2[:, :]), in1=ei, op=ALU.mult), reads=[bW2, bE, bZR], writes=[bZR])
                            S.op("dve", lambda e: e.scalar_tensor_tensor(out=XI[:], in0=W1[:], scalar=-1.0, in1=ZR[:],
                                                                         op0=ALU.mult, op1=ALU.subtract), reads=[bW1, bZR, bX], writes=[bX])
                            first = (g4 == 0 and d_ == 0)
                            last = (g4 == 3 and d_ == 1)
                            mm(psh(2, hf), cm_t[sl][:, (d_ * 2) * 128:(d_ * 2 + 1) * 128], XR[:], first, False,
                               [bbm[sl], bX], [bPS[2][hf]], inc=False)
                            mm(psh(2, hf), cm_t[sl][:, (d_ * 2 + 1) * 128:(d_ * 2 + 2) * 128], XI[:], False, last,
                               [bbm[sl], bX], [bPS[2][hf]], inc=True)
                for hf in range(2):
                    hs = slice(hf * 512, (hf + 1) * 512)
                    S.op("dve", lambda e: e.scalar_tensor_tensor(out=ytmp[:], in0=OS[:, kt, hs], scalar=SM("ssmd", l * 6 + kt, 1),
                                                                 in1=psh(2, hf), op0=ALU.mult, op1=ALU.add),
                         reads=[bOS[kt], bC, bPS[2][hf], byt], writes=[byt])
                    S.op("dve", lambda e: e.tensor_tensor(out=W1[:], in0=ytmp[:], in1=ytmp[:], op=ALU.mult), reads=[byt, bW1], writes=[bW1])
                    S.op("dve", lambda e: e.tensor_scalar(out=W1[:], in0=W1[:], scalar1=0.044715, scalar2=1.0, op0=ALU.mult,
                                                          op1=ALU.add), reads=[bW1], writes=[bW1])
                    S.op("dve", lambda e: e.tensor_tensor(out=W1[:], in0=W1[:], in1=ytmp[:], op=ALU.mult), reads=[bW1, byt], writes=[bW1])
                    S.op("act", lambda e: e.activation(out=W1[:], in_=W1[:], func=AF.Sigmoid, scale=2.0 * 0.7978845608),
                         reads=[bW1], writes=[bW1])
                    S.op("dve", lambda e: e.tensor_tensor(out=YS[:, kt, hs], in0=W1[:], in1=ytmp[:], op=ALU.mult),
                         reads=[bW1, byt], writes=[bYS[kt]])
            for _ in modgen:
                pass
            S.dma("sp", o_fin[:, l * 384:(l + 1) * 384], fin[:], reads=[bFin], writes=[bOut])
            for m0 in range(0, 6, 2):
                wp, wb = load_w([(d_wglu[l][:, m0 * 128:(m0 + 2) * 128], 0)], 6, 256)
                for j in range(2):
                    mt = m0 + j
                    for hf in range(2):
                        hs = slice(hf * 512, (hf + 1) * 512)
                        for k6 in range(6):
                            mm(psh(0, hf), wp[:, k6, j * 128:(j + 1) * 128], YS[:, k6, hs], k6 == 0, k6 == 5,
                               [wb, bYS[k6]], [bPS[0][hf]])
                        S.op("act", lambda e: e.activation(out=ytmp[:], in_=psh(0, hf), func=AF.Sigmoid,
                                                           bias=SM("bglu", l * 6 + mt, 1)), reads=[bPS[0][hf], bC, byt], writes=[byt])
                        S.op("dve", lambda e: e.tensor_tensor(out=OS[:, mt, hs], in0=ytmp[:], in1=YS[:, mt, hs], op=ALU.mult),
                             reads=[byt, bYS[mt]], writes=[bOS[mt]])

        def rope_tile(tile, bt_, mt, scale, stk_tmp):
            (t1, t2, bt1, ropet) = stk_tmp
            for hf in range(2):
                hs = slice(hf * 512, (hf + 1) * 512)
                mm(psh(0, hf), pswap[:], tile[:, mt, hs], True, True, [bC, bt_], [bPS[0][hf]])
                S.op("dve", lambda e: e.scalar_tensor_tensor(out=t1[:], in0=tile[:, mt, hs], scalar=scale,
                                                             in1=ropet[:, hf * 512:(hf + 1) * 512], op0=ALU.mult, op1=ALU.mult),
                     reads=[bt_, bC, bt1], writes=[bt1])
                S.op("dve", lambda e: e.scalar_tensor_tensor(out=t2[:], in0=psh(0, hf), scalar=scale,
                                                             in1=ropet[:, T + hf * 512:T + (hf + 1) * 512], op0=ALU.mult, op1=ALU.mult),
                     reads=[bPS[0][hf], bC, bt1], writes=[bt1])
                S.op("dve", lambda e: e.tensor_tensor(out=tile[:, mt, hs], in0=t1[:], in1=t2[:], op=ALU.add),
                     reads=[bt1], writes=[bt_])

        def win_branch(l, OW, bOW, wk):
            KW = sb("KW", [128, 4, T], BF16, wk)
            bKW = [Buf() for _ in range(4)]
            VW = sb("VW", [128, 8 * 256 + 64], BF16, wk)
            VWv = VW[:, 0:2048].rearrange("p (b c) -> p b c", b=8)
            bVW = Buf()
            KC = sb("KWc", [128, 4, 512], BF16, wk)
            VC = sb("VWc", [128, 4 * 256 + 64], BF16, wk)
            bKC, bVC = Buf(), Buf()
            pt = [sb("wpt%d" % i, [128, 1152], BF16, wk) for i in range(2)]
            bpt = [Buf(), Buf()]
            rinv = [sb("wri%d" % i, [128, 128], F32, wk) for i in range(2)]
            brinv = [Buf(), Buf()]
            t1 = sb("wt1", [128, 512], F32, wk)
            t2 = sb("wt2", [128, 512], F32, wk)
            bt1 = Buf()
            ropet = sb("ropet", [128, 2 * T], BF16, wk)
            wmask = sb("wmask_s", [128, 24 * 128], BF16, wk)
            S.dma("pool", ropet[:], d_rope[:, :], writes=[bC])
            S.dma("pool", wmask[:], d_wmask[:, :], writes=[bC])
            stg = ([sb("wstg%d" % i, [128, 256], F32, wk) for i in range(2)], [Buf(), Buf()])
            S.op("pool", lambda e: e.memset(VW[:, 2048:2112], 1.0), writes=[bVW])
            S.op("pool", lambda e: e.memset(VC[:, 1024:1088], 1.0), writes=[bVC])
            S.dma("pool", KC[:], d_kwc[l].rearrange("(k p) c -> p k c", p=128), writes=[bKC])
            S.dma("pool", VC[:, 0:1024].rearrange("p (b c) -> p b c", b=4), d_vwc[l].rearrange("(b p) c -> p b c", p=128),
                  writes=[bVC])
            S.op("act", lambda e: e.activation(out=SM("sinke", 0, 12), in_=SM("sink", l * 12, 12), func=AF.Exp),
                 reads=[bC, bSmall], writes=[bSmall])

            def ev_q(mt, hf, ps, pb):
                S.op("act", lambda e: e.activation(out=OW[:, mt, hf * 512:(hf + 1) * 512], in_=ps, func=AF.Copy),
                     reads=[pb], writes=[bOW[mt]])

            def ev_k(mt, hf, ps, pb):
                S.op("act", lambda e: e.activation(out=KW[:, mt, hf * 512:(hf + 1) * 512], in_=ps, func=AF.Copy),
                     reads=[pb], writes=[bKW[mt]])
            proj_fm(l, [[(OFF_QW + m * 128, 128)] for m in range(6)], ev_q)
            proj_fm(l, [[(OFF_KW + j * 64, 64), (OFF_KW + j * 64, 64)] for j in range(4)], ev_k)
            if STOP < 4.07:
                return
            tokmajor(l, OFF_KW, 256, 0, None, stg)
            tokmajor(l, OFF_VW, 256, 256, (VWv, bVW, 0), stg)
            if STOP < 4.15:
                return
            for mt in range(6):
                rope_tile(OW, bOW[mt], mt, 0.125, (t1, t2, bt1, ropet))
            for mt in range(4):
                rope_tile(KW, bKW[mt], mt, 1.0, (t1, t2, bt1, ropet))
            if STOP < 4.25:
                return
            def mk_win(h):
                kv = h // 3
                qt, qh = h // 2, h % 2

                def units(n):
                    u = []
                    for rel in (-1, 0, 1):
                        b = n + rel
                        if 0 <= b < 8:
                            u.append((b, [wmask[:, (n * 3 + rel + 1) * 128:(n * 3 + rel + 2) * 128]]))
                    return u
                return attn_head(OW, bOW[qt], qt, qh, KW, bKW[kv], kv, KC, bKC, kv,
                                 lambda b: vaug_ap(VW, b * 256 + kv * 64, 2048), lambda cb: vaug_ap(VC, cb * 256 + kv * 64, 1024),
                                 bVW, bVC, units, SM("sinke", h, 1), (pt, bpt, rinv, brinv))
            run_pipeline([(lambda h=h: mk_win(h)) for h in range(12)])

        def na_branch(l, hg, ON, bON, nk):
            KN = sb("KN", [128, 3, T], BF16, nk)
            bKN = [Buf() for _ in range(3)]
            VN = sb("VN", [128, 8 * 384 + 64], BF16, nk)
            VNv = VN[:, 0:3072].rearrange("p (b c) -> p b c", b=8)
            bVN = Buf()
            KC = sb("KNc", [128, 3, 512], BF16, nk)
            VC = sb("VNc", [128, 4 * 384 + 64], BF16, nk)
            bKC, bVC = Buf(), Buf()
            RR = [sb("nar%d" % i, [128, 7 * 128], BF16, nk) for i in range(2)]
            bRR = [Buf(), Buf()]
            pt = [sb("npt%d" % i, [128, 1152], BF16, nk) for i in range(2)]
            bpt = [Buf(), Buf()]
            rinv = [sb("nri%d" % i, [128, 128], F32, nk) for i in range(2)]
            brinv = [Buf(), Buf()]
            nmask = sb("nmask_s", [128, N_NAM * 128], BF16, nk)
            S.dma("pool", nmask[:], d_nmask[:, :], writes=[bC])
            stg = ([sb("nstg%d" % i, [128, 256], F32, nk) for i in range(2)], [Buf(), Buf()])
            S.op("pool", lambda e: e.memset(VN[:, 3072:3136], 1.0), writes=[bVN])
            S.op("pool", lambda e: e.memset(VC[:, 1536:1600], 1.0), writes=[bVC])
            S.dma("pool", KC[:], d_knc[l][hg * 384:(hg + 1) * 384, :].rearrange("(k p) c -> p k c", p=128), writes=[bKC])
            S.dma("pool", VC[:, 0:1536].rearrange("p (b c) -> p b c", b=4),
                  d_vnc[l][:, hg * 384:(hg + 1) * 384].rearrange("(b p) c -> p b c", p=128), writes=[bVC])

            def ev_q(mt, hf, ps, pb):
                S.op("act", lambda e: e.activation(out=ON[:, hg * 3 + mt, hf * 512:(hf + 1) * 512], in_=ps, func=AF.Copy,
                                                   scale=0.125), reads=[pb], writes=[bON[hg * 3 + mt]])

            def ev_k(mt, hf, ps, pb):
                S.op("act", lambda e: e.activation(out=KN[:, mt, hf * 512:(hf + 1) * 512], in_=ps, func=AF.Copy),
                     reads=[pb], writes=[bKN[mt]])
            proj_fm(l, [[(OFF_QN + (hg * 3 + m) * 128, 128)] for m in range(3)], ev_q)
            proj_fm(l, [[(OFF_KN + (hg * 3 + m) * 128, 128)] for m in range(3)], ev_k)
            for c0 in (0, 256):
                w = 256 if c0 == 0 else 128
                tokmajor(l, OFF_KN + hg * 384 + c0, w, 512 + hg * 384 + c0, None, stg)
                tokmajor(l, OFF_VN + hg * 384 + c0, w, 1280 + hg * 384 + c0, (VNv, bVN, c0), stg)
            def mk_na(hl):
                h = hg * 6 + hl
                qt, qh = h // 2, h % 2
                lt = hl // 2
                sl = hl % 2
                S.dma("pool", RR[sl][:], d_nar[l][h], writes=[bRR[sl]])

                def units(n, sl=sl):
                    u = []
                    for b in NA_KB[n]:
                        idx = NA_IDX[(n, b)]
                        dl = b - n + 3
                        u.append((b, [nmask[:, idx * 128:(idx + 1) * 128], RR[sl][:, dl * 128:(dl + 1) * 128]]))
                    return u
                return attn_head_na(ON, bON[qt], qt, qh, KN, bKN[lt], lt, KC, bKC, lt,
                                    lambda b, hl=hl: vaug_ap(VN, b * 384 + hl * 64, 3072),
                                    lambda cb, hl=hl: vaug_ap(VC, cb * 384 + hl * 64, 1536),
                                    bVN, bVC, units, None, (pt, bpt, rinv, brinv), bRR[sl])
            run_pipeline([(lambda hl=hl: mk_na(hl)) for hl in range(6)])

        def attn_head_na(*args):
            extra = args[-1]
            old = attn_extra[0]
            attn_extra[0] = extra
            r = attn_head(*args[:-1])
            attn_extra[0] = old
            return r

        attn_extra = [None]
        _mm_orig = mm

        def mm(out, lhsT, rhs, start, stop, reads, writes, inc=None):
            if attn_extra[0] is not None:
                reads = list(reads) + [attn_extra[0]]
            _mm_orig(out, lhsT, rhs, start, stop, reads, writes, inc)

        def merge(l, OS, bOS, OW, bOW, ON, bON, gk):
            MT = [sb("mT%d" % i, [128, 4, T], BF16, gk) for i in range(2)]
            bMT = [Buf(), Buf()]
            acc = sb("macc", [128, T], F32, gk)
            bacc = Buf()
            sg = sb("msg", [128, T], F32, gk)
            bsg = Buf()
            OB = [(OS, bOS), (OW, bOW), (ON, bON)]
            for fg in range(4):
                mi = fg % 2
                for fl in range(4):
                    ft = fg * 4 + fl
                    for x in range(3):
                        wp, wb = load_w([(d_win[l][:, OFF_G + x * D + ft * 128:OFF_G + x * D + (ft + 1) * 128], 0)], 16, 128)
                        wq, wqb = load_w([(d_wbr[l][x][:, ft * 128:(ft + 1) * 128], 0)], 6, 128)
                        O_, bO_ = OB[x]
                        for hf in range(2):
                            hs = slice(hf * 512, (hf + 1) * 512)
                            for kt in range(16):
                                mm(psh(0, hf), wp[:, kt, :], HT[:, kt, hs], kt == 0, kt == 15, [wb, bHT[kt]], [bPS[0][hf]])
                            for k6 in range(6):
                                mm(psh(1, hf), wq[:, k6, :], O_[:, k6, hs], k6 == 0, k6 == 5, [wqb, bO_[k6]], [bPS[1][hf]])
                        S.op("act", lambda e: e.activation(out=sg[:], in_=PS[0][:, :], func=AF.Sigmoid), reads=bPS[0] + [bsg],
                             writes=[bsg])
                        if x == 0:
                            S.op("dve", lambda e: e.tensor_tensor(out=acc[:], in0=PS[1][:, :], in1=sg[:], op=ALU.mult),
                                 reads=bPS[1] + [bsg, bacc], writes=[bacc])
                        else:
                            S.op("dve", lambda e: e.tensor_tensor(out=sg[:], in0=PS[1][:, :], in1=sg[:], op=ALU.mult),
                                 reads=bPS[1] + [bsg], writes=[bsg])
                            if x == 1:
                                S.op("dve", lambda e: e.tensor_tensor(out=acc[:], in0=acc[:], in1=sg[:], op=ALU.add),
                                     reads=[bsg, bacc], writes=[bacc])
                            else:
                                S.op("dve", lambda e: e.tensor_tensor(out=MT[mi][:, fl, :], in0=acc[:], in1=sg[:], op=ALU.add),
                                     reads=[bsg, bacc], writes=[bMT[mi]])
                for mc in range(2):
                    wp, wb = load_w([(d_wout[l][fg * 512:(fg + 1) * 512, mc * 1024:(mc + 1) * 1024], 0)], 4, 1024)
                    for mj in range(8):
                        mo = mc * 8 + mj
                        for hf in range(2):
                            hs = slice(hf * 512, (hf + 1) * 512)
                            for k4 in range(4):
                                mm(psh(2, hf), wp[:, k4, mj * 128:(mj + 1) * 128], MT[mi][:, k4, hs], k4 == 0, k4 == 3,
                                   [wb, bMT[mi]], [bPS[2][hf]])
                        S.op("dve", lambda e: e.scalar_tensor_tensor(out=XT[:, mo, :], in0=PS[2][:, :], scalar=SM("modv%d" % (l % 2), 32 + mo, 1),
                                                                     in1=XT[:, mo, :], op0=ALU.mult, op1=ALU.add),
                             reads=bPS[2] + [bSmall, bXT[mo]], writes=[bXT[mo]])

        def ffn(l, fk):
            HID = [sb("hid%d" % i, [128, 11, T], BF16, fk) for i in range(2)]
            bHID = [Buf(), Buf()]
            acc_a = sb("facc_a", [128, T], F32, fk)
            acc_b = sb("facc_b", [128, T], F32, fk)
            ba, bb = Buf(), Buf()
            cw = lambda k, ch: big2[:, CW0 + (l * 3 + k) * 88 + ch:CW0 + (l * 3 + k) * 88 + ch + 1]
            cb = lambda ch: big2[:, CB0 + l * 88 + ch:CB0 + l * 88 + ch + 1]
            S.op("dve", lambda e: e.tensor_scalar(out=SM("nw0", 0, 88), in0=big2[:, CW0 + (l * 3) * 88:CW0 + (l * 3 + 1) * 88],
                                                  scalar1=SM("flags", 1, 1), scalar2=None, op0=ALU.mult), reads=[bC, bSmall], writes=[bSmall])
            S.op("dve", lambda e: e.tensor_scalar(out=SM("nw2", 0, 88), in0=big2[:, CW0 + (l * 3 + 2) * 88:CW0 + (l * 3 + 3) * 88],
                                                  scalar1=SM("flags", 1, 1), scalar2=None, op0=ALU.mult), reads=[bC, bSmall], writes=[bSmall])

            def bcols(ap2d, start):
                c = ap2d[:, start:start + 1]
                (ps_, pc), (fs, fc) = c.ap
                return AP(c.tensor, c.offset, [[ps_, pc], [256 * fs, 3]])

            for g in range(4):
                hi = g % 2
                for il in range(11):
                    i = g * 11 + il
                    wp, wb = load_w([(d_wup[l][:, i * 128:(i + 1) * 128], 0), (d_wup[l][:, DFF + i * 128:DFF + (i + 1) * 128], 128)], 16, 256)
                    for ab in range(2):
                        ch = i + 44 * ab
                        pi = ab
                        for hf in range(2):
                            hs = slice(hf * 512, (hf + 1) * 512)
                            for kt in range(16):
                                mm(psh(pi, hf), wp[:, kt, ab * 128:(ab + 1) * 128], HT[:, kt, hs], kt == 0, kt == 15,
                                   [wb, bHT[kt]], [bPS[pi][hf]])
                        acc, bacc = (acc_a, ba) if ab == 0 else (acc_b, bb)
                        S.op("act", lambda e: e.activation(out=acc[:], in_=PS[pi][:, :], func=AF.Identity, scale=cw(1, ch), bias=cb(ch)),
                             reads=bPS[pi] + [bC, bacc], writes=[bacc])
                        S.op("dve", lambda e: e.scalar_tensor_tensor(out=acc[:, 1:T], in0=PS[pi][:, 0:T - 1], scalar=cw(0, ch),
                                                                     in1=acc[:, 1:T], op0=ALU.mult, op1=ALU.add),
                             reads=bPS[pi] + [bC, bacc], writes=[bacc])
                        S.op("dve", lambda e: e.scalar_tensor_tensor(out=acc[:, 0:T - 1], in0=PS[pi][:, 1:T], scalar=cw(2, ch),
                                                                     in1=acc[:, 0:T - 1], op0=ALU.mult, op1=ALU.add),
                             reads=bPS[pi] + [bC, bacc], writes=[bacc])
                        S.op("dve", lambda e: e.scalar_tensor_tensor(out=bcols(acc[:, :], 256), in0=bcols(PS[pi][:, :], 255),
                                                                     scalar=SM("nw0", ch, 1), in1=bcols(acc[:, :], 256),
                                                                     op0=ALU.mult, op1=ALU.add),
                             reads=bPS[pi] + [bSmall, bacc], writes=[bacc])
                        S.op("dve", lambda e: e.scalar_tensor_tensor(out=bcols(acc[:, :], 255), in0=bcols(PS[pi][:, :], 256),
                                                                     scalar=SM("nw2", ch, 1), in1=bcols(acc[:, :], 255),
                                                                     op0=ALU.mult, op1=ALU.add),
                             reads=bPS[pi] + [bSmall, bacc], writes=[bacc])
                    S.op("act", lambda e: e.activation(out=acc_a[:], in_=acc_a[:], func=AF.Silu), reads=[ba], writes=[ba])
                    S.op("dve", lambda e: e.tensor_tensor(out=HID[hi][:, il, :], in0=acc_a[:], in1=acc_b[:], op=ALU.mult),
                         reads=[ba, bb], writes=[bHID[hi]])
                for mc in range(8):
                    wp, wb = load_w([(d_wdn[l][g * 1408:(g + 1) * 1408, mc * 256:(mc + 1) * 256], 0)], 11, 256)
                    for mj in range(2):
                        mo = mc * 2 + mj
                        for hf in range(2):
                            hs = slice(hf * 512, (hf + 1) * 512)
                            for k in range(11):
                                mm(psh(2, hf), wp[:, k, mj * 128:(mj + 1) * 128], HID[hi][:, k, hs], k == 0, k == 10,
                                   [wb, bHID[hi]], [bPS[2][hf]])
                        S.op("dve", lambda e: e.scalar_tensor_tensor(out=XT[:, mo, :], in0=PS[2][:, :], scalar=SM("modv%d" % (l % 2), 80 + mo, 1),
                                                                     in1=XT[:, mo, :], op0=ALU.mult, op1=ALU.add),
                             reads=bPS[2] + [bSmall, bXT[mo]], writes=[bXT[mo]])

        for l in range(depth):
            layer(l)
        with contextlib.ExitStack() as stk:
            phase_norm("fg", 0, None, 0, stk, final=True)
            S.barrier()
        print("instructions:", S.n_ins, {k: v for k, v in S.cnt.items()})
    return nc


def _consts():
    ident = np.eye(128, dtype=np.float32)
    psw = np.zeros((128, 128), np.float32)
    for p in range(128):
        i = p % 64
        j = i + 16 if (i % 32) < 16 else i - 16
        psw[(p // 64) * 64 + j, p] = 1.0
    ones = np.ones((128, 128), np.float32)
    tt = np.tile(np.arange(1, 257, dtype=np.float32)[None], (128, 1))
    rm = np.ones((128, T), np.float32)
    rm[:, 0::256] = 0.0
    return np.concatenate([ident, psw, ones, tt, rm], axis=1)


def _rope_tables(sample):
    cos = np.ones((128, T), np.float32)
    sin = np.zeros((128, T), np.float32)
    if sample:
        t = np.arange(T)
        row = (t // 64).astype(np.float32)
        colp = (t % 64).astype(np.float32)
        inv = (10000.0 ** (-np.arange(16, dtype=np.float32) / 16)).astype(np.float32)
        for p in range(128):
            i = p % 64
            j = i % 16
            pos = row if i < 32 else colp
            ang = (pos * inv[j]).astype(np.float32)
            cos[p] = np.cos(ang)
            sgn = -1.0 if (i % 32) < 16 else 1.0
            sin[p] = sgn * np.sin(ang)
    return np.concatenate([cos, sin], axis=1).astype(np.float32)


def _win_masks(sample):
    m = np.full((24, 128, 128), NEG, np.float32)
    k = np.arange(128)[:, None]
    q = np.arange(128)[None, :]
    for n in range(8):
        for rel in (-1, 0, 1):
            b = n + rel
            if not (0 <= b < 8):
                continue
            if sample:
                if rel == 0:
                    ok = np.ones((128, 128), bool)
                elif rel == -1:
                    ok = q <= k
                else:
                    ok = k <= q
            else:
                ok = np.full((128, 128), (b // 2) == (n // 2))
            m[n * 3 + rel + 1] = np.where(ok, 0.0, NEG)
    return np.ascontiguousarray(m.transpose(1, 0, 2).reshape(128, 24 * 128))


def _na_masks(sample):
    m = np.full((N_NAM, 128, 128), NEG, np.float32)
    for (j, b), idx in NA_IDX.items():
        for ka in range(2):
            for qa in range(2):
                rq, rk = 2 * j + qa, 2 * b + ka
                if sample:
                    rs = min(max(rq - 4, 0), 8)
                    ok = rs <= rk < rs + 8
                else:
                    ok = (rq // 4) == (rk // 4)
                if ok:
                    m[idx, ka * 64:(ka + 1) * 64, qa * 64:(qa + 1) * 64] = 0.0
    return np.ascontiguousarray(m.transpose(1, 0, 2).reshape(128, N_NAM * 128))


def _na_r(rpb_l):
    out = np.zeros((12, 7, 128, 128), np.float32)
    ck = np.arange(64)[:, None]
    cq = np.arange(64)[None, :]
    qs = np.clip(cq - 8, 0, 48)
    valid = (ck >= qs) & (ck < qs + 16)
    dc = np.clip(ck - cq + 15, 0, 30)
    for dl in range(7):
        for ka in range(2):
            for qa in range(2):
                dr = 2 * (dl - 3) + ka - qa
                if -7 <= dr <= 7:
                    blk = np.where(valid[None], rpb_l[:, dr + 7][:, dc], NEG)
                else:
                    blk = np.full((12, 64, 64), NEG, np.float32)
                out[:, dl, ka * 64:(ka + 1) * 64, qa * 64:(qa + 1) * 64] = blk
    return np.ascontiguousarray(out.transpose(0, 2, 1, 3).reshape(12, 128, 7 * 128))


def _pcol(a):
    a = np.asarray(a, np.float32)
    lead = a.shape[:-1]
    n = a.shape[-1] // 128
    b = a.reshape(lead + (n, 128))
    b = np.moveaxis(b, -1, 0)
    return np.ascontiguousarray(b.reshape(128, -1))


def _ssm_state_layout(a):
    a = np.asarray(a, np.float32)
    lead = a.shape[:-2]
    b = a.reshape(lead + (24, 2, 64))
    b = np.moveaxis(b, -3, -1)
    b = b.reshape(lead + (128, 24))
    b = np.moveaxis(b, -2, 0)
    return np.ascontiguousarray(b)


def prepare(inputs, depth=DEPTH):
    f = lambda k: np.asarray(inputs[k], np.float32)
    shared = {}
    shared["w_mod"] = f("w_mod")[:depth]
    shared["b_mod"] = _pcol(f("b_mod")[:depth])
    shared["n1g"] = _pcol(f("norm1_g")[:depth])
    shared["n2g"] = _pcol(f("norm2_g")[:depth])
    shared["fg"] = _pcol(f("final_g"))
    shared["w_in"] = f("w_in")[:depth]
    shared["w_branch"] = f("w_branch")[:depth]
    shared["w_out"] = f("w_out")[:depth]
    shared["w_up"] = f("w_up")[:depth]
    shared["w_down"] = f("w_down")[:depth]
    shared["w_glu"] = f("w_glu")[:depth]
    shared["b_glu"] = _pcol(f("b_glu")[:depth])
    shared["conv_w"] = _pcol(f("conv_w")[:depth])
    shared["conv_b"] = _pcol(f("conv_b")[:depth])
    shared["ssm_d"] = _pcol(f("ssm_d")[:depth].reshape(depth, 768))
    shared["sink"] = np.ascontiguousarray(np.tile(f("win_sink")[:depth].reshape(1, depth * 12), (128, 1)))
    shared["lam_re"] = _ssm_state_layout(f("ssm_lam_re")[:depth]).reshape(128, depth * 48)
    shared["lam_im"] = _ssm_state_layout(f("ssm_lam_im")[:depth]).reshape(128, depth * 48)
    ldt = np.tile(f("ssm_log_dt")[:depth][..., None], (1, 1, 1, 64))
    shared["log_dt"] = _ssm_state_layout(ldt).reshape(128, depth * 48)
    bre, bim = f("ssm_b_re")[:depth], f("ssm_b_im")[:depth]
    cre, cim = f("ssm_c_re")[:depth], f("ssm_c_im")[:depth]
    bmat = np.zeros((depth, 24, 128, 4, 128), np.float32)
    cmat = np.zeros((depth, 24, 128, 4, 128), np.float32)
    for gp in range(24):
        for gl in range(2):
            g = gp * 2 + gl
            r0 = (gp % 4) * 32 + gl * 16
            for d_ in range(2):
                for ri, (bb, cc) in enumerate(((bre, cre), (bim, cim))):
                    bmat[:, gp, r0:r0 + 16, d_ * 2 + ri, gl * 64:(gl + 1) * 64] = bb[:, d_, g].transpose(0, 2, 1)
                    cmat[:, gp, gl * 64:(gl + 1) * 64, d_ * 2 + ri, r0:r0 + 16] = cc[:, d_, g].transpose(0, 2, 1)
    shared["bmat"] = bmat.reshape(depth, 24, 128, 512)
    shared["cmat"] = cmat.reshape(depth, 24, 128, 512)
    shared["consts"] = _consts()
    rpb = f("na_rpb")[:depth]
    nar_s = np.stack([_na_r(rpb[l]) for l in range(depth)])
    nar_p = np.zeros_like(nar_s)
    per_type = {}
    for sample in (False, True):
        per_type[sample] = {"rope": _rope_tables(sample), "wmask": _win_masks(sample), "nmask": _na_masks(sample),
                            "nar": nar_s if sample else nar_p}
    xp, xs = f("x_prompt"), f("x_sample")
    in_maps = []
    for c in range(8):
        sample = c >= 4
        m = dict(shared)
        m.update(per_type[sample])
        flags = np.zeros((128, 4), np.float32)
        if sample:
            b = c - 4
            m["xT"] = np.ascontiguousarray(xs[b].T)
            m["cond"] = _pcol(f("c")[b])
            flags[:, 0] = 1.0
            hre = _ssm_state_layout(f("state_ssm_re")[b, :depth])
            him = _ssm_state_layout(f("state_ssm_im")[b, :depth])
            m["h0"] = np.ascontiguousarray(np.stack([hre, him], axis=-1).reshape(128, -1))
            kw = f("cache_win_k")[b, :depth].reshape(depth, 512, 4, 64)
            kw = np.repeat(kw.transpose(0, 2, 3, 1)[:, :, None], 2, axis=2)
            m["kwc"] = np.ascontiguousarray(kw.reshape(depth, 512, 512))
            m["knc"] = np.ascontiguousarray(f("cache_na_k")[b, :depth].reshape(depth, 512, 768).transpose(0, 2, 1))
            m["vwc"] = np.ascontiguousarray(f("cache_win_v")[b, :depth].reshape(depth, 512, 256))
            m["vnc"] = np.ascontiguousarray(f("cache_na_v")[b, :depth].reshape(depth, 512, 768))
        else:
            m["xT"] = np.ascontiguousarray(xp[4 * c:4 * c + 4].reshape(T, D).T)
            m["cond"] = _pcol(f("c_ctx"))
            flags[:, 1] = -1.0
            flags[:, 2] = NEG
            m["h0"] = np.zeros((128, depth * 96), np.float32)
            m["kwc"] = np.zeros((depth, 512, 512), np.float32)
            m["knc"] = np.zeros((depth, 768, 512), np.float32)
            m["vwc"] = np.zeros((depth, 512, 256), np.float32)
            m["vnc"] = np.zeros((depth, 512, 768), np.float32)
        m["flags"] = flags
        in_maps.append(m)
    return in_maps


def assemble(results, depth=DEPTH):
    yp = np.zeros((16, 256, D), np.float32)
    ys = np.zeros((4, T, D), np.float32)
    nwk = np.zeros((16, depth, 256, 4, 64), np.float32)
    nwv = np.zeros_like(nwk)
    nnk = np.zeros((16, depth, 256, 12, 64), np.float32)
    nnv = np.zeros_like(nnk)
    sre = np.zeros((16, depth, 2, 48, 64), np.float32)
    sim = np.zeros_like(sre)
    for c in range(8):
        r = results[c]
        y = np.asarray(r["yT"]).T
        if c >= 4:
            ys[c - 4] = y
            continue
        yp[4 * c:4 * c + 4] = y.reshape(4, 256, D)
        kv = np.asarray(r["kv"]).reshape(depth, 4, 256, 2048).transpose(1, 0, 2, 3)
        nwk[4 * c:4 * c + 4] = kv[..., 0:256].reshape(4, depth, 256, 4, 64)
        nwv[4 * c:4 * c + 4] = kv[..., 256:512].reshape(4, depth, 256, 4, 64)
        nnk[4 * c:4 * c + 4] = kv[..., 512:1280].reshape(4, depth, 256, 12, 64)
        nnv[4 * c:4 * c + 4] = kv[..., 1280:2048].reshape(4, depth, 256, 12, 64)
        fin = np.asarray(r["fin"]).reshape(2, 64, depth, 4, 2, 24, 2)
        fin = fin.transpose(3, 2, 4, 5, 0, 1, 6).reshape(4, depth, 2, 48, 64, 2)
        sre[4 * c:4 * c + 4] = fin[..., 0]
        sim[4 * c:4 * c + 4] = fin[..., 1]
    return (yp, ys, nwk, nwv, nnk, nnv, sre, sim)


_NC_CACHE = {}


def kernel(**inputs):
    if DEPTH not in _NC_CACHE:
        _NC_CACHE[DEPTH] = build_nc(DEPTH)
    nc = _NC_CACHE[DEPTH]
    in_maps = prepare(inputs, DEPTH)
    res = run_bass_kernel_spmd(nc, in_maps, core_ids=list(range(8)))
    del in_maps
    return assemble(res.results, DEPTH)
```
